# Optimizing a Trainium2 kernel written in Bass

```python
import math
import jax, jax.numpy as jnp
from jax import lax
import numpy as np

D_MODEL = 2048
BATCH = 32
SEQ = 256
DEPTH = 2
DEC_BATCH = 8
DEC_SEQ = 1024
PAST_LEN = 512

GRID_W = 64
N_FNO_LAYERS = (DEPTH + 1) // 2
N_SSD_LAYERS = DEPTH // 2
FNO_WIDTH = 2 * D_MODEL
FNO_GROUPS = 16
FNO_GROUP_DIM = FNO_WIDTH // FNO_GROUPS
SSD_WIDTH = 2 * D_MODEL
SSD_HEAD_DIM = 64
SSD_HEADS = SSD_WIDTH // SSD_HEAD_DIM
SSD_GROUPS = 8
SSD_D_STATE = 128
SSD_CONV_W = 5
SSD_CONV_DIM = SSD_WIDTH + 2 * SSD_GROUPS * SSD_D_STATE
SSD_IN_DIM = SSD_WIDTH + SSD_CONV_DIM + 2 * SSD_HEADS
CHUNK = 128
DEEPNORM_ALPHA = (2 * DEPTH) ** 0.25
DEEPNORM_BETA = (8 * DEPTH) ** -0.25
LN_EPS = 1e-5

kernel_name = "fnet_ssd_hybrid_diffusion_step"


def _layer_norm(x, g=None, b=None):
    xf = x.astype(jnp.float32)
    mu = jnp.mean(xf, axis=-1, keepdims=True)
    var = jnp.mean(jnp.square(xf - mu), axis=-1, keepdims=True)
    y = (xf - mu) * lax.rsqrt(var + LN_EPS)
    if g is not None:
        y = y * g.astype(jnp.float32) + b.astype(jnp.float32)
    return y.astype(x.dtype)


def _rms_norm(x, w):
    xf = x.astype(jnp.float32)
    y = xf * lax.rsqrt(jnp.mean(jnp.square(xf), axis=-1, keepdims=True) + LN_EPS)
    return (y * w.astype(jnp.float32)).astype(x.dtype)


def _sincos_1d(pos, dim):
    omega = 1.0 / (10000.0 ** (jnp.arange(dim // 2, dtype=jnp.float32) / (dim / 2)))
    ang = pos.astype(jnp.float32)[:, None] * omega[None, :]
    return jnp.concatenate([jnp.sin(ang), jnp.cos(ang)], axis=-1)


def _grid_pos_embed(n_tokens, dim):
    t = jnp.arange(n_tokens)
    row, col = t // GRID_W, t % GRID_W
    return jnp.concatenate([_sincos_1d(row, dim // 2), _sincos_1d(col, dim // 2)], axis=-1)


def _centred_dwconv(u, w, bias):
    pad = SSD_CONV_W // 2
    L = u.shape[1]
    up = jnp.pad(u, ((0, 0), (pad, pad), (0, 0)))
    acc = bias
    for k in range(SSD_CONV_W):
        acc = acc + up[:, k:k + L] * w[k]
    return acc


def _segsum_exp(a_cs):
    T = a_cs.shape[-1]
    diff = a_cs[..., :, None] - a_cs[..., None, :]
    mask = jnp.tril(jnp.ones((T, T), dtype=bool))
    return jnp.where(mask, jnp.exp(jnp.where(mask, diff, 0.0)), 0.0)


def _ssd_scan(x, dt, a, bm, cm, init):
    b, L, H, P = x.shape
    G, N = bm.shape[2], bm.shape[3]
    R = H // G
    nc = L // CHUNK
    x = x.astype(jnp.float32)
    dt = dt.astype(jnp.float32)
    xdt = (x * dt[..., None]).reshape(b, nc, CHUNK, G, R, P)
    da = (dt * a.astype(jnp.float32)).reshape(b, nc, CHUNK, G, R).transpose(0, 3, 4, 1, 2)
    bm = bm.astype(jnp.float32).reshape(b, nc, CHUNK, G, N)
    cm = cm.astype(jnp.float32).reshape(b, nc, CHUNK, G, N)
    a_cs = jnp.cumsum(da, axis=-1)
    lmat = _segsum_exp(a_cs)
    y_diag = jnp.einsum('bclgn,bcsgn,bgrcls,bcsgrp->bclgrp', cm, bm, lmat, xdt)
    decay_states = jnp.exp(a_cs[..., -1:] - a_cs)
    states = jnp.einsum('bclgn,bgrcl,bclgrp->bcgrpn', bm, decay_states, xdt)
    init = init.astype(jnp.float32).reshape(b, 1, G, R, P, N)
    states = jnp.concatenate([init, states], axis=1)
    chunk_tot = jnp.pad(a_cs[..., -1], ((0, 0), (0, 0), (0, 0), (1, 0)))
    decay_chunk = _segsum_exp(jnp.cumsum(chunk_tot, axis=-1))
    new_states = jnp.einsum('bgrzc,bcgrpn->bzgrpn', decay_chunk, states)
    states_in, final = new_states[:, :-1], new_states[:, -1]
    y_off = jnp.einsum('bclgn,bcgrpn,bgrcl->bclgrp', cm, states_in, jnp.exp(a_cs))
    y = (y_diag + y_off).reshape(b, L, H, P)
    return y, final.reshape(b, H, P, N)


def _fourier_mixer(h, w_in, w_out):
    b, L, _ = h.shape
    uz = h @ w_in
    u, z = uz[..., :FNO_WIDTH], uz[..., FNO_WIDTH:]
    u = u.reshape(b, L, FNO_GROUPS, FNO_GROUP_DIM).astype(jnp.float32)
    y = jnp.fft.fftn(u, axes=(1, 3), norm='ortho').real
    y = y.reshape(b, L, FNO_WIDTH).astype(h.dtype) * jax.nn.silu(z)
    return y @ w_out


def _ssd_mixer(h, init, w_in, conv_w, conv_b, dt_bias, a_log, d_skip, norm_w, w_out):
    b, L, _ = h.shape
    proj = h @ w_in
    z = proj[..., :SSD_WIDTH]
    xbc = proj[..., SSD_WIDTH:SSD_WIDTH + SSD_CONV_DIM]
    dt_raw = proj[..., SSD_WIDTH + SSD_CONV_DIM:].reshape(b, L, 2, SSD_HEADS)
    xbc = jax.nn.silu(_centred_dwconv(xbc, conv_w, conv_b))
    gn = SSD_GROUPS * SSD_D_STATE
    xs = xbc[..., :SSD_WIDTH].reshape(b, L, SSD_HEADS, SSD_HEAD_DIM)
    bm = xbc[..., SSD_WIDTH:SSD_WIDTH + gn].reshape(b, L, SSD_GROUPS, SSD_D_STATE)
    cm = xbc[..., SSD_WIDTH + gn:].reshape(b, L, SSD_GROUPS, SSD_D_STATE)
    dt = jax.nn.softplus(dt_raw.astype(jnp.float32) + dt_bias.astype(jnp.float32))
    a = -jnp.exp(a_log.astype(jnp.float32))
    y_f, s_f = _ssd_scan(xs, dt[:, :, 0], a[0], bm, cm, init[:, 0])
    flip = lambda t: jnp.flip(t, axis=1)
    y_b, s_b = _ssd_scan(flip(xs), flip(dt[:, :, 1]), a[1], flip(bm), flip(cm), init[:, 1])
    y = y_f + flip(y_b) + d_skip.astype(jnp.float32)[:, None] * xs.astype(jnp.float32)
    y = y.reshape(b, L, SSD_WIDTH).astype(h.dtype) * jax.nn.silu(z)
    y = _rms_norm(y, norm_w)
    return y @ w_out, jnp.stack([s_f, s_b], axis=1)


def _trunk(x, cond, init_states, w_ada, b_ada, ln_g, ln_b, fno_w_in, fno_w_out,
           ssd_w_in, ssd_conv_w, ssd_conv_b, ssd_dt_bias, ssd_a_log, ssd_d, ssd_norm_w, ssd_w_out):
    finals = []
    for i in range(DEPTH):
        mod = jax.nn.silu(cond) @ w_ada[i] + b_ada[i]
        shift = mod[:, None, :D_MODEL]
        scale = mod[:, None, D_MODEL:2 * D_MODEL]
        gate = mod[:, None, 2 * D_MODEL:]
        h = _layer_norm(x) * (1.0 + scale) + shift
        j = i // 2
        if i % 2 == 0:
            out = _fourier_mixer(h, fno_w_in[j], fno_w_out[j])
        else:
            out, fin = _ssd_mixer(h, init_states[:, j], ssd_w_in[j], ssd_conv_w[j], ssd_conv_b[j],
                                  ssd_dt_bias[j], ssd_a_log[j], ssd_d[j], ssd_norm_w[j], ssd_w_out[j])
            finals.append(fin)
        x = _layer_norm(DEEPNORM_ALPHA * x + gate * out, ln_g[i], ln_b[i])
    return x, finals


def setup_inputs(seed: int = 0) -> dict:
    key = jax.random.key(seed)
    ks = jax.random.split(key, 24)
    f32 = jnp.float32
    nrm = lambda k, s, sc: jax.random.normal(k, s, f32) * sc
    dt0 = jnp.exp(jax.random.uniform(ks[14], (N_SSD_LAYERS, 2, SSD_HEADS), f32,
                                     math.log(1e-3), math.log(1e-1)))
    return {
        "x_prompt": nrm(ks[0], (BATCH, SEQ, D_MODEL), 1.0),
        "x_sample": nrm(ks[1], (DEC_BATCH, DEC_SEQ, D_MODEL), 1.0),
        "state_ssd_ctx": nrm(ks[2], (DEC_BATCH, N_SSD_LAYERS, 2, SSD_HEADS, SSD_HEAD_DIM, SSD_D_STATE), 0.1),
        "c": nrm(ks[3], (DEC_BATCH, D_MODEL), 1.0),
        "c_ctx": nrm(ks[4], (D_MODEL,), 1.0),
        "w_ada": nrm(ks[5], (DEPTH, D_MODEL, 3 * D_MODEL), 0.5 * D_MODEL ** -0.5),
        "b_ada": nrm(ks[6], (DEPTH, 3 * D_MODEL), 0.02),
        "ln_g": 1.0 + nrm(ks[7], (DEPTH, D_MODEL), 0.02),
        "ln_b": nrm(ks[8], (DEPTH, D_MODEL), 0.02),
        "fno_w_in": nrm(ks[9], (N_FNO_LAYERS, D_MODEL, 2 * FNO_WIDTH), D_MODEL ** -0.5),
        "fno_w_out": nrm(ks[10], (N_FNO_LAYERS, FNO_WIDTH, D_MODEL), DEEPNORM_BETA * FNO_WIDTH ** -0.5),
        "ssd_w_in": nrm(ks[11], (N_SSD_LAYERS, D_MODEL, SSD_IN_DIM), D_MODEL ** -0.5),
        "ssd_conv_w": nrm(ks[12], (N_SSD_LAYERS, SSD_CONV_W, SSD_CONV_DIM), SSD_CONV_W ** -0.5),
        "ssd_conv_b": nrm(ks[13], (N_SSD_LAYERS, SSD_CONV_DIM), 0.02),
        "ssd_dt_bias": dt0 + jnp.log(-jnp.expm1(-dt0)),
        "ssd_a_log": jnp.log(jax.random.uniform(ks[15], (N_SSD_LAYERS, 2, SSD_HEADS), f32, 1.0, 16.0)),
        "ssd_d": 1.0 + nrm(ks[16], (N_SSD_LAYERS, SSD_HEADS), 0.1),
        "ssd_norm_w": 1.0 + nrm(ks[17], (N_SSD_LAYERS, SSD_WIDTH), 0.02),
        "ssd_w_out": nrm(ks[18], (N_SSD_LAYERS, SSD_WIDTH, D_MODEL), DEEPNORM_BETA * SSD_WIDTH ** -0.5),
    }


def reference(x_prompt, x_sample, state_ssd_ctx, c, c_ctx, w_ada, b_ada, ln_g, ln_b,
              fno_w_in, fno_w_out, ssd_w_in, ssd_conv_w, ssd_conv_b, ssd_dt_bias, ssd_a_log,
              ssd_d, ssd_norm_w, ssd_w_out):
    weights = (w_ada, b_ada, ln_g, ln_b, fno_w_in, fno_w_out, ssd_w_in, ssd_conv_w, ssd_conv_b,
               ssd_dt_bias, ssd_a_log, ssd_d, ssd_norm_w, ssd_w_out)
    zero_init = jnp.zeros((x_prompt.shape[0], N_SSD_LAYERS, 2, SSD_HEADS, SSD_HEAD_DIM, SSD_D_STATE),
                          dtype=jnp.float32)
    y_prompt, finals = _trunk(x_prompt, c_ctx[None, :], zero_init, *weights)
    state_ssd = jnp.stack(finals, axis=1).astype(x_prompt.dtype)
    n_lat = x_sample.shape[1]
    pos = _grid_pos_embed(n_lat, D_MODEL).astype(x_sample.dtype)
    y_sample, _ = _trunk(x_sample + pos[None], c, state_ssd_ctx, *weights)
    return (y_prompt, y_sample, state_ssd)
```

```python
import math
from contextlib import ExitStack

import numpy as np
import ml_dtypes

import concourse.bass as bass
import concourse.mybir as mybir
from concourse.bass_utils import run_bass_kernel_spmd

F32 = mybir.dt.float32
BF16 = mybir.dt.bfloat16
ALU = mybir.AluOpType
AF = mybir.ActivationFunctionType

ENGS = ("pe", "act", "dve", "pool", "sp")
DMA_POOL = {"sp": 14, "act": 4, "pool": 10}

D = 2048
NT = 8
ALPHA = 4.0 ** 0.25
EPS = 1e-5
DEPTH_RUN = 2


class _Op:
    __slots__ = ("eng", "fn", "deps", "dma", "needed", "val", "sem", "prev_same_sem")

    def __init__(self, eng, fn, dma):
        self.eng = eng
        self.fn = fn
        self.deps = []
        self.dma = dma
        self.needed = False
        self.val = None
        self.sem = None
        self.prev_same_sem = None


class Prog:
    def __init__(self, nc):
        self.nc = nc
        self.ops = {e: [] for e in ENGS}
        self.last_w = {}
        self.readers = {}
        self.dma_count = {e: 0 for e in DMA_POOL}
        self.dma_last = {}
        self.last_compute = {}
        self.bar = None
        self.bar_seen = set()

    def op(self, eng, fn, reads=(), writes=(), dma=False):
        o = _Op(eng, fn, dma)
        deps = {}
        for k in reads:
            w = self.last_w.get(k)
            if w is not None:
                deps[id(w)] = w
        for k in writes:
            w = self.last_w.get(k)
            if w is not None:
                deps[id(w)] = w
            for r in self.readers.get(k, {}).values():
                deps[id(r)] = r
        ring_load = (dma and eng == "pool" and not reads and len(writes) > 0
                     and all(isinstance(k, tuple) and k[0] == "ring" for k in writes))
        if self.bar is not None and eng not in self.bar_seen and not ring_load:
            self.bar_seen.add(eng)
            deps[id(self.bar)] = self.bar
        for d in deps.values():
            if d is o:
                continue
            if d.eng == "pe" and eng == "pe" and not d.dma and not dma:
                continue
            d.needed = True
            o.deps.append(d)
        if dma:
            n = self.dma_count[eng]
            self.dma_count[eng] = n + 1
            slot = (eng, n % DMA_POOL[eng])
            o.sem = slot
            o.prev_same_sem = self.dma_last.get(slot)
            o.val = 16 * (n // DMA_POOL[eng] + 1)
            self.dma_last[slot] = o
            o.needed = True
            rk = slot
        else:
            self.last_compute[eng] = o
            rk = eng
        for k in writes:
            self.last_w[k] = o
            self.readers[k] = {}
        for k in reads:
            if k not in writes:
                self.readers.setdefault(k, {})[rk] = o
        self.ops[eng].append(o)
        return o

    def barrier(self):
        b = _Op("sp", lambda e: e.nop(), False)
        deps = list(self.last_compute.values()) + list(self.dma_last.values())
        for d in deps:
            d.needed = True
            b.deps.append(d)
        b.needed = True
        self.ops["sp"].append(b)
        self.last_compute["sp"] = b
        self.bar = b
        self.bar_seen = {"sp"}
        keep_w = {k: v for k, v in self.last_w.items() if isinstance(k, tuple) and k[0] == "ring"}
        keep_r = {k: v for k, v in self.readers.items() if isinstance(k, tuple) and k[0] == "ring"}
        self.last_w.clear()
        self.readers.clear()
        self.last_w.update(keep_w)
        self.readers.update(keep_r)

    def emit(self, sems, dsems):
        for e in ENGS:
            c = 0
            for o in self.ops[e]:
                if o.dma:
                    continue
                if o.needed:
                    c += 1
                    o.val = c
                    o.sem = e
        final_waits = [(o.sem, o.val) for o in self.dma_last.values()]

        def run(e, eng):
            waited = {}

            def wait(semkey, val):
                if waited.get(semkey, 0) >= val:
                    return
                waited[semkey] = val
                s = dsems[semkey] if isinstance(semkey, tuple) else sems[semkey]
                eng.wait_ge(s, val)

            for o in self.ops[e]:
                need = {}
                for d in o.deps:
                    if need.get(d.sem, 0) < d.val:
                        need[d.sem] = d.val
                if o.dma and o.prev_same_sem is not None:
                    p = o.prev_same_sem
                    if need.get(p.sem, 0) < p.val:
                        need[p.sem] = p.val
                for k, v in need.items():
                    wait(k, v)
                ins = o.fn(eng)
                if o.dma:
                    ins.then_inc(dsems[o.sem], 16)
                elif o.needed:
                    ins.then_inc(sems[e], 1)
            if e == "sp":
                for k, v in final_waits:
                    wait(k, v)

        return run


def build_and_emit(nc, prog):
    with ExitStack() as st:
        sems = {e: st.enter_context(nc.semaphore("s_" + e)) for e in ENGS}
        dsems = {}
        for e, n in DMA_POOL.items():
            for i in range(n):
                dsems[(e, i)] = st.enter_context(nc.semaphore("d_%s_%d" % (e, i)))
        run = prog.emit(sems, dsems)
        block = st.enter_context(nc.Block())

        @block.tensor
        def _(eng):
            run("pe", eng)

        @block.scalar
        def _(eng):
            run("act", eng)

        @block.vector
        def _(eng):
            run("dve", eng)

        @block.gpsimd
        def _(eng):
            run("pool", eng)

        @block.sync
        def _(eng):
            run("sp", eng)


def build_nc():
    nc = bass.Bass("TRN2", target_bir_lowering=False)
    P = Prog(nc)

    def din(name, shape, dt=F32):
        return nc.dram_tensor(name, shape, dt, kind="ExternalInput").ap()

    def dout(name, shape, dt=F32):
        return nc.dram_tensor(name, shape, dt, kind="ExternalOutput").ap()

    xa = din("xa", [1024, D])
    xb = din("xb", [1024, D])
    pos = din("pos", [1024, D])
    st_in = din("st_in", [2 * 64 * 64, 128])
    cT = din("cT", [128, 16, 2])
    bT = din("bT", [128, 2, 48])
    w_ada = din("w_ada", [2, D, 3 * D])
    ln_g = din("ln_g", [2, D])
    ln_b = din("ln_b", [2, D])
    fno_w_in = din("fno_w_in", [D, 4 * D])
    fno_w_out = din("fno_w_out", [2 * D, D])
    ssd_w_in = din("ssd_w_in", [D, 10368])
    ssd_w_out = din("ssd_w_out", [2 * D, D])
    convw = din("convw", [128, 48, 5])
    convb = din("convb", [128, 48])
    dtb = din("dtb", [1, 128])
    alog = din("alog", [1, 128])
    dsk = din("dsk", [1, 64])
    normw = din("normw", [128, 32])
    c_identb = din("c_identb", [128, 128], BF16)
    c_identf = din("c_identf", [128, 128])
    c_tfb = din("c_tfb", [128, 128], BF16)
    c_tbb = din("c_tbb", [128, 128], BF16)
    c_tff = din("c_tff", [128, 128])
    c_tbf = din("c_tbf", [128, 128])
    c_negf = din("c_negf", [128, 128], BF16)
    c_negb = din("c_negb", [128, 128], BF16)
    c_maskf = din("c_maskf", [128, 128], BF16)
    c_maskb = din("c_maskb", [128, 128], BF16)
    c_onesf = din("c_onesf", [128, 128])
    c_ntfb = din("c_ntfb", [128, 128], BF16)
    c_ntbb = din("c_ntbb", [128, 128], BF16)
    c_csc = din("c_csc", [128, 2, 512], BF16)
    c_dft256 = din("c_dft256", [128, 2, 2, 256], BF16)
    c_dft1024 = din("c_dft1024", [2, 1024, 1024], BF16)

    ya = dout("ya", [1024, D])
    yb = dout("yb", [1024, D])
    so = dout("so", [4 * 2 * 64 * 64, 128])
    ysc = nc.dram_tensor("ysc", [4096, 1024], BF16, kind="Internal").ap()
    yfs = nc.dram_tensor("yfs", [1024, 512], F32, kind="Internal").ap()
    yfs2 = nc.dram_tensor("yfs2", [1024, 512], F32, kind="Internal").ap()

    off = [16512]

    def sb(name, shape, dt, at=None):
        nbytes = int(np.prod(shape[1:])) * (4 if dt == F32 else 2)
        nbytes = (nbytes + 31) // 32 * 32
        if at is None:
            o = off[0]
            off[0] += nbytes
        else:
            o = at
        assert o + nbytes <= 229376, (name, o, nbytes)
        return nc.alloc_sbuf_tensor_at(name, list(shape), dt, offset=o)

    X1 = sb("X1", [128, NT, D], F32)
    HT = sb("HT", [128, 16, 1024], BF16)
    RING = [sb("ring%d" % i, [128, 4096], BF16) for i in range(4)]
    identb = sb("identb", [128, 128], BF16)
    identf = sb("identf", [128, 128], F32)
    tfb = sb("tfb", [128, 128], BF16)
    tbb = sb("tbb", [128, 128], BF16)
    tff = sb("tff", [128, 128], F32)
    tbf = sb("tbf", [128, 128], F32)
    negf = sb("negf", [128, 128], BF16)
    negb = sb("negb", [128, 128], BF16)
    maskf = sb("maskf", [128, 128], BF16)
    maskb = sb("maskb", [128, 128], BF16)
    onesf = sb("onesf", [128, 128], F32)
    ntfb = sb("ntfb", [128, 128], BF16)
    ntbb = sb("ntbb", [128, 128], BF16)
    modT = sb("modT", [128, 2, 48, 2], F32)
    epst = sb("epst", [128, 1], F32)
    _save = off[0]
    off[0] = 228352
    cTs = sb("cTs", [128, 16, 2], F32)
    scT = sb("scT", [128, 16, 2], BF16)
    bTs = sb("bTs", [128, 2, 48], F32)
    off[0] = _save
    PH = None

    def phase_alloc():
        off[0] = PH2

    ps = nc.alloc_psum_tensor("ps", [128, 8, 512], F32)

    def dma(q, out, in_, reads=(), writes=()):
        return P.op(q, lambda e: e.dma_start(out=out, in_=in_), reads, writes, dma=True)

    def mm(out, lhsT, rhs, start, stop, reads, writes):
        return P.op("pe", lambda e: e.matmul(out, lhsT=lhsT, rhs=rhs, start=start, stop=stop), reads, writes)

    def tr(out, in_, ident, reads, writes):
        return P.op("pe", lambda e: e.transpose(out, in_, ident), reads, writes)

    def act(out, in_, func, reads, writes, bias=None, scale=None):
        kw = {}
        if bias is not None:
            kw["bias"] = bias
        if scale is not None:
            kw["scale"] = scale
        return P.op("act", lambda e: e.activation(out=out, in_=in_, func=func, **kw), reads, writes)

    def tt(eng, out, in0, in1, op, reads, writes):
        return P.op(eng, lambda e: e.tensor_tensor(out=out, in0=in0, in1=in1, op=op), reads, writes)

    def ts(eng, out, in0, s1, s2, op0, op1, reads, writes):
        if s2 is None:
            return P.op(eng, lambda e: e.tensor_scalar(out=out, in0=in0, scalar1=s1, scalar2=None, op0=op0), reads, writes)
        return P.op(eng, lambda e: e.tensor_scalar(out=out, in0=in0, scalar1=s1, scalar2=s2, op0=op0, op1=op1), reads, writes)

    def stt(out, in0, scalar, in1, op0, op1, reads, writes):
        return P.op("dve", lambda e: e.scalar_tensor_tensor(out=out, in0=in0, scalar=scalar, in1=in1, op0=op0, op1=op1),
                    reads, writes)

    def cp(eng, out, in_, reads, writes):
        if eng == "act":
            return P.op("act", lambda e: e.copy(out=out, in_=in_), reads, writes)
        return P.op(eng, lambda e: e.tensor_copy(out=out, in_=in_), reads, writes)

    ring_ctr = [0]

    def ring_next():
        i = ring_ctr[0] % 4
        ring_ctr[0] += 1
        return i

    bank_ctr = [0]

    def bank_next(n=8):
        b = bank_ctr[0] % n
        bank_ctr[0] += 1
        return b

    def wload(src_ap, view_shape, cast=True):
        i = ring_next()
        n = int(np.prod(view_shape[1:]))
        v = RING[i][:, 0:n]
        if len(view_shape) == 3:
            v = v.rearrange("p (a b) -> p a b", a=view_shape[1])
        dma("pool" if cast else "sp", v, src_ap, writes=[("ring", i)])
        return i, v

    for t, s, k in ((identb, c_identb, "identb"), (identf, c_identf, "identf"), (tfb, c_tfb, "tfb"), (tbb, c_tbb, "tbb"),
                    (tff, c_tff, "tff"), (tbf, c_tbf, "tbf"), (negf, c_negf, "negf"), (negb, c_negb, "negb"),
                    (maskf, c_maskf, "maskf"), (maskb, c_maskb, "maskb"), (onesf, c_onesf, "onesf"), (ntfb, c_ntfb, "ntfb"), (ntbb, c_ntbb, "ntbb"),
                    (cTs, cT, "cTs"), (bTs, bT, "bTs")):
        dma("sp", t[:], s, writes=[k])
    CONSTS = ["identb", "identf", "tfb", "tbb", "tff", "tbf", "negf", "negb", "maskf", "maskb", "onesf", "csc",
              "dft256", "modT", "epst"]
    P.op("dve", lambda e: e.memset(epst[:], EPS), writes=["epst"])

    import os
    MKM = int(os.environ.get("MK_M", "99"))
    if MKM >= 1:
        act(scT[:], cTs[:], AF.Silu, ["cTs"], ["scT"])
    def m_slab(i, sl, bank):
        wv = w_ada[i].rearrange("(k p) n -> p k n", p=128)
        slot, v = wload(wv[:, :, sl * 256:(sl + 1) * 256], [128, 16, 256])
        for mmi in range(2):
            m = 2 * sl + mmi
            for k in range(16):
                mm(ps[:, bank, 2 * m:2 * m + 2], v[:, k, mmi * 128:(mmi + 1) * 128], scT[:, k, :], k == 0, k == 15,
                   [("ring", slot), "scT"], [("ps", bank)])

    def m_final(i, bank):
        tt("dve", modT[:, i, :, :], ps[:, bank, 0:96].rearrange("p (m c) -> p m c", c=2),
           bTs[:, i, :].unsqueeze(2).to_broadcast([128, 48, 2]), ALU.add, ["bTs"], [("ps", bank), "modT"])
        ts("dve", modT[:, i, 16:32, :], modT[:, i, 16:32, :], 1.0, None, ALU.add, None, ["modT"], ["modT"])

    for sl in range(24):
        m_slab(0, sl, 0)
    m_final(0, 0)

    def phase_L(layer, unit):
        phase_alloc()
        ci = unit
        xsrc = xa if unit == 0 else xb
        xn = [sb("xn%d" % i, [128, D], BF16) for i in range(2)]
        pt = [sb("pt%d" % i, [128, D], F32) for i in range(2)] if (layer == 0 and unit == 1) else None
        stat_tiles = []
        for i in range(2):
            stat_tiles.append(dict(
                st6=sb("Lst6%d" % i, [128, 4, 6], F32), mv=sb("Lmv%d" % i, [128, 2], F32),
                sd=sb("Lsd%d" % i, [128, 1], F32), rstd=sb("Lrstd%d" % i, [128, 1], F32),
                nmr=sb("Lnmr%d" % i, [128, 1], F32)))
        def L_prep(t):
            b = t % 2
            xk = ("X1", t)
            x_ap = X1[:, t, :]
            if layer == 0:
                dma("sp", x_ap, xsrc[t * 128:(t + 1) * 128, :], writes=[xk])
                if unit == 1:
                    dma("sp", pt[b][:], pos[t * 128:(t + 1) * 128, :], writes=[("pt", b)])
                    tt("dve", x_ap, x_ap, pt[b][:], ALU.add, [xk, ("pt", b)], [xk])
            S = stat_tiles[b]
            kk = ("Lst", b)
            for c in range(4):
                P.op("dve", lambda e, c=c, S=S, x_ap=x_ap: e.bn_stats(out=S["st6"][:, c, :], in_=x_ap[:, c * 512:(c + 1) * 512]),
                     [xk], [(kk, "st6")])
            P.op("dve", lambda e, S=S: e.bn_aggr(out=S["mv"][:], in_=S["st6"][:].rearrange("p a b -> p (a b)")),
                 [(kk, "st6")], [(kk, "mv")])
            act(S["sd"][:], S["mv"][:, 1:2], AF.Sqrt, [(kk, "mv"), "epst"], [(kk, "sd")], bias=epst[:, 0:1], scale=1.0)
            P.op("dve", lambda e, S=S: e.reciprocal(out=S["rstd"][:], in_=S["sd"][:]), [(kk, "sd")], [(kk, "rstd")])
            stt(S["nmr"][:], S["mv"][:, 0:1], -1.0, S["rstd"][:], ALU.mult, ALU.mult, [(kk, "mv"), (kk, "rstd")], [(kk, "nmr")])
            act(xn[b][:], x_ap, AF.Identity, [xk, (kk, "rstd"), (kk, "nmr")], [("xn", b)],
                bias=S["nmr"][:, 0:1], scale=S["rstd"][:, 0:1])
            for kc in range(16):
                bk = 4 + 2 * (t % 2) + (kc // 8)
                pv = ps[:, bk, :].bitcast(BF16)
                src = pv[:, (kc % 8) * 128:(kc % 8 + 1) * 128]
                tr(src, xn[b][:, kc * 128:(kc + 1) * 128], identb[:], [("xn", b), "identb"], [("ps", bk)])

        def L_evac(t):
            b = t % 2
            for kc in range(16):
                bk = 4 + 2 * (t % 2) + (kc // 8)
                pv = ps[:, bk, :].bitcast(BF16)
                src = pv[:, (kc % 8) * 128:(kc % 8 + 1) * 128]
                dst = HT[:, kc, t * 128:(t + 1) * 128]
                sc = modT[:, layer, 16 + kc, ci:ci + 1]
                sh = modT[:, layer, kc, ci:ci + 1]
                if kc < 8:
                    ts("dve", dst, src, sc, sh, ALU.mult, ALU.add, ["modT"], [("ps", bk), ("HT", t, kc)])
                else:
                    act(dst, src, AF.Identity, ["modT"], [("ps", bk), ("HT", t, kc)], bias=sh, scale=sc)

        L_prep(0)
        for t in range(NT):
            if t + 1 < NT:
                L_prep(t + 1)
            L_evac(t)
        P.barrier()

    HT_ALL = [("HT", t) for t in range(NT)]

    def phase_F(unit):
        phase_alloc()
        UT = [sb("UT%d" % i, [128, 2, 1024], BF16) for i in range(2)]
        ZT = [sb("ZT%d" % i, [128, 2, 1024], BF16) for i in range(2)]
        AA = [sb("AA%d" % i, [128, NT, 512], BF16) for i in range(2)]
        YS = [sb("YS%d" % i, [128, 512], BF16) for i in range(4)]
        csc = sb("csc", [128, 2, 512], BF16)
        dft256 = sb("dft256", [128, 2, 2, 256], BF16)
        dma("sp", csc[:], c_csc, writes=["csc"])
        dma("sp", dft256[:], c_dft256, writes=["dft256"])
        ys_ctr = 0
        wv = fno_w_in.rearrange("(k p) n -> p k n", p=128)
        dfv = c_dft1024.rearrange("c (k p) q -> c p k q", p=128)
        nb_F = 7 if unit == 0 else 8
        for g in range(16):
            gb = g % 2
            if unit == 0 and g >= 1 and g <= 12:
                m_slab(1, 2 * (g - 1), 7)
                m_slab(1, 2 * (g - 1) + 1, 7)
                if g == 12:
                    m_final(1, 7)
            su, vu = wload(wv[:, :, g * 256:(g + 1) * 256], [128, 16, 256])
            sz, vz = wload(wv[:, :, 4096 + g * 256:4096 + (g + 1) * 256], [128, 16, 256])
            for which, slot, v in (("u", su, vu), ("z", sz, vz)):
                for m in range(2):
                    for n in range(2):
                        bk = bank_next(nb_F)
                        for k in range(16):
                            mm(ps[:, bk, :], v[:, k, m * 128:(m + 1) * 128], HT[:, k, n * 512:(n + 1) * 512], k == 0, k == 15,
                               [("ring", slot)] + [("HT", t_, k) for t_ in range(n * 4, n * 4 + 4)], [("ps", bk)])
                        if which == "u":
                            cp("act", UT[gb][:, m, n * 512:(n + 1) * 512], ps[:, bk, :], [], [("ps", bk), ("UT", gb, m, n)])
                        else:
                            act(ZT[gb][:, m, n * 512:(n + 1) * 512], ps[:, bk, :], AF.Silu, [], [("ps", bk), ("ZT", gb, m, n)])
            for t in range(NT):
                bk = bank_next(nb_F)
                n = t // 4
                for m in range(2):
                    mm(ps[:, bk, :], UT[gb][:, m, t * 128:(t + 1) * 128], csc[:, m, :], m == 0, m == 1,
                       [("UT", gb, m, n), "csc"], [("ps", bk)])
                cp("dve", AA[gb][:, t, :], ps[:, bk, :], [], [("ps", bk), ("AA", gb, t)])
            if unit == 1:
                slabs = {}
                for n in range(2):
                    for c in range(2):
                        slabs[(n, c)] = wload(dfv[c][:, :, n * 512:(n + 1) * 512], [128, 8, 512], cast=False)
            for n in range(2):
                for m in range(2):
                    bk = bank_next(nb_F)
                    if unit == 1:
                        idx = 0
                        for kp in range(8):
                            for c in range(2):
                                slot, v = slabs[(n, c)]
                                mm(ps[:, bk, :], AA[gb][:, kp, c * 256 + m * 128: c * 256 + (m + 1) * 128], v[:, kp, :],
                                   idx == 0, idx == 15, [("AA", gb, kp), ("ring", slot)], [("ps", bk)])
                                idx += 1
                    else:
                        for sq in range(2):
                            s = n * 2 + sq
                            idx = 0
                            for kp in range(2):
                                t = s * 2 + kp
                                for c in range(2):
                                    mm(ps[:, bk, sq * 256:(sq + 1) * 256],
                                       AA[gb][:, t, c * 256 + m * 128: c * 256 + (m + 1) * 128], dft256[:, kp, c, :],
                                       idx == 0, idx == 3, [("AA", gb, t), "dft256"], [("ps", bk)])
                                    idx += 1
                    yi = ys_ctr % 4
                    ys_ctr += 1
                    tt("dve", YS[yi][:], ps[:, bk, :], ZT[gb][:, m, n * 512:(n + 1) * 512], ALU.mult,
                       [("ZT", gb, m, n)], [("ps", bk), ("YS", yi)])
                    kidx = g * 2 + m
                    dma("sp", ysc[kidx * 128:(kidx + 1) * 128, n * 512:(n + 1) * 512], YS[yi][:],
                        [("YS", yi)], [("ysc", kidx, n)])
        P.barrier()

    def phase_O(layer, unit, w_out, rstd_tile):
        phase_alloc()
        ci = unit
        YTH = sb("YTH", [128, 16, 1024], BF16)
        gate_b = sb("gate_b", [128, D], F32)
        tmp = [sb("otmp%d" % i, [128, 512], F32) for i in range(2)]
        GB = sb("GB", [128, D], F32)
        BB = sb("BB", [128, D], F32)
        dg = [sb("dg%d" % i, [128, 128], F32) for i in range(2)]
        stat_tiles = []
        for i in range(2):
            stat_tiles.append(dict(
                st6=sb("Ost6%d" % i, [128, 4, 6], F32), mv=sb("Omv%d" % i, [128, 2], F32),
                sd=sb("Osd%d" % i, [128, 1], F32), rstd=sb("Orstd%d" % i, [128, 1], F32),
                nmr=sb("Onmr%d" % i, [128, 1], F32)))
        dma("sp", GB[:], ln_g[layer:layer + 1, :].to_broadcast([128, D]), writes=["GB"])
        dma("sp", BB[:], ln_b[layer:layer + 1, :].to_broadcast([128, D]), writes=["BB"])
        for kc in range(16):
            d = dg[kc % 2]
            ts("dve", d[:], identf[:], modT[:, layer, 32 + kc, ci:ci + 1], None, ALU.mult, None, ["identf", "modT"], [("dg", kc % 2)])
            bk = kc // 4
            mm(ps[:, bk, (kc % 4) * 128:(kc % 4 + 1) * 128], onesf[:], d[:], True, True, ["onesf", ("dg", kc % 2)], [("ps", bk)])
            if kc % 4 == 3:
                cp("dve", gate_b[:, bk * 512:(bk + 1) * 512], ps[:, bk, :], [], [("ps", bk), "gate_b"])
        for t in range(NT):
            P.op("act", lambda e, t=t: e.mul(out=X1[:, t, :], in_=X1[:, t, :], mul=ALPHA), [("X1", t)], [("X1", t)])
        wv = w_out.rearrange("(k p) n -> p k n", p=128)
        yv = ysc.rearrange("(k p) t -> p k t", p=128)
        tctr = 0
        for hh in range(2):
            for q in range(4):
                dma("sp", YTH[:, q * 4:(q + 1) * 4, :], yv[:, hh * 16 + q * 4: hh * 16 + (q + 1) * 4, :],
                    [("ysc", hh * 16 + q * 4 + i, n_) for i in range(4) for n_ in range(2)], [("YTH", q)])
            for j in range(4):
                slabs = [wload(wv[:, hh * 16 + s * 8: hh * 16 + (s + 1) * 8, j * 512:(j + 1) * 512], [128, 8, 512]) for s in range(2)]
                for t in [None]:
                    pass
                for tset in range(2):
                  for k in range(16):
                    slot, v = slabs[k // 8]
                    for t in range(tset * 4, tset * 4 + 4):
                        mm(ps[:, t, :], YTH[:, k, t * 128:(t + 1) * 128], v[:, k % 8, :], k == 0, k == 15,
                           [("YTH", k // 4), ("ring", slot)], [("ps", t)])
                  for t in range(tset * 4, tset * 4 + 4):
                    tb = tctr % 2
                    tctr += 1
                    sc = 1.0 if rstd_tile is None else rstd_tile[:, t:t + 1]
                    rk = [] if rstd_tile is None else ["rmsr"]
                    stt(tmp[tb][:], ps[:, t, :], sc, gate_b[:, j * 512:(j + 1) * 512], ALU.mult, ALU.mult,
                        ["gate_b"] + rk, [("ps", t), ("otmp", tb)])
                    tt("dve", X1[:, t, j * 512:(j + 1) * 512], X1[:, t, j * 512:(j + 1) * 512], tmp[tb][:], ALU.add,
                       [("X1", t), ("otmp", tb)], [("X1", t)])
        ydst = ya if unit == 0 else yb

        def O_stats(t):
            b = t % 2
            S = stat_tiles[b]
            kk = ("Ost", b)
            xk = ("X1", t)
            x_ap = X1[:, t, :]
            for c in range(4):
                P.op("dve", lambda e, c=c, S=S, x_ap=x_ap: e.bn_stats(out=S["st6"][:, c, :], in_=x_ap[:, c * 512:(c + 1) * 512]),
                     [xk], [(kk, "st6")])
            P.op("dve", lambda e, S=S: e.bn_aggr(out=S["mv"][:], in_=S["st6"][:].rearrange("p a b -> p (a b)")),
                 [(kk, "st6")], [(kk, "mv")])

        def O_norm(t):
            b = t % 2
            S = stat_tiles[b]
            kk = ("Ost", b)
            xk = ("X1", t)
            x_ap = X1[:, t, :]
            act(S["sd"][:], S["mv"][:, 1:2], AF.Sqrt, [(kk, "mv"), "epst"], [(kk, "sd")], bias=epst[:, 0:1], scale=1.0)
            P.op("dve", lambda e, S=S: e.reciprocal(out=S["rstd"][:], in_=S["sd"][:]), [(kk, "sd")], [(kk, "rstd")])
            stt(S["nmr"][:], S["mv"][:, 0:1], -1.0, S["rstd"][:], ALU.mult, ALU.mult, [(kk, "mv"), (kk, "rstd")], [(kk, "nmr")])
            act(x_ap, x_ap, AF.Identity, [xk, (kk, "rstd"), (kk, "nmr")], [xk], bias=S["nmr"][:, 0:1], scale=S["rstd"][:, 0:1])

        def O_affine(t):
            xk = ("X1", t)
            x_ap = X1[:, t, :]
            tt("dve", x_ap, x_ap, GB[:], ALU.mult, [xk, "GB"], [xk])
            tt("dve", x_ap, x_ap, BB[:], ALU.add, [xk, "BB"], [xk])
            if layer == DEPTH_RUN - 1:
                dma("sp", ydst[t * 128:(t + 1) * 128, :], x_ap, [xk], [("yout", t)])

        O_stats(0)
        for t in range(NT):
            O_norm(t)
            if t + 1 < NT:
                O_stats(t + 1)
            O_affine(t)
        P.barrier()


    RMSR = sb("RMSR", [128, NT], F32)
    SSQ = sb("SSQ", [128, NT, 8], F32)
    PH2 = off[0]

    def phase_S(unit):
        off[0] = PH2
        S_ = 4 if unit == 0 else 1
        L_ = 1024 // S_
        NCH = L_ // 128
        wv = ssd_w_in.rearrange("(k p) n -> p k n", p=128)
        LDH = sb("LDH", [128, NT, 128], BF16)
        LDL = sb("LDL", [128, NT, 128], BF16)
        DTt = sb("DTt", [128, 128], F32)
        DAb = sb("DAb", [128, NT, 128], BF16)
        NACS = sb("NACS", [128, 1, 128], F32)
        OFFD = sb("OFFD", [128, NT, 128], F32)
        CDEC = sb("CDEC", [128, NT, 128], F32)
        WST = sb("WST", [128, NT, 128], F32)
        dtb_b = sb("dtb_b", [128, 128], F32)
        A_b = sb("A_b", [128, 128], F32)
        dsk_b = sb("dsk_b", [128, 64], F32)
        DAf = [sb("DAf0", [128, 128], F32)] * 2
        TOTs = [sb("TOT0", [128, 128], F32)] * 2
        cvw = sb("cvw", [128, 48, 5], F32)
        cvb = sb("cvb", [128, 48], F32)
        nrw = sb("nrw", [128, 32], F32)
        PC = sb("PC", [128, S_ * (L_ + 4)], F32)
        ACC = sb("ACC", [128, 1024], F32)
        XCT1 = sb("XCT1", [128, 1024], BF16)
        BT = sb("BT", [128, 1024], BF16)
        CT = sb("CT", [128, 1024], BF16)
        XS = sb("XS", [128, NT, 512], BF16)
        Bt = sb("Bt", [128, NT, 128], BF16)
        ST = sb("ST", [128, 512], F32)
        Sbf = [sb("Sbf%d" % i, [128, 512], BF16) for i in range(2)]
        YFL1 = sb("YFL1", [128, 512], F32)
        alias_base = off[0]
        XDTD = [sb("XDTD%d" % i, [128, 512], BF16) for i in range(2)]
        Gd = [sb("Gd%d" % i, [128, 128], BF16) for i in range(2)]
        Lt = sb("Lt", [128, 1024], BF16)
        MT = [sb("MT%d" % i, [128, 1024], BF16) for i in range(2)]
        TMP = sb("TMPy", [128, 512], F32)
        YA = [sb("YA%d" % i, [128, 512], F32) for i in range(2)]
        YFL = sb("YFL", [128, 512], F32)
        STL = TMP[:, :].rearrange("p (q n) -> p q n", q=4)
        YFLs = [YFL, YFL1]
        YFLk = ["YFL", "YFL1"]
        SZ = [sb("SZ%d" % i, [128, 512], BF16) for i in range(2)]
        YG = [sb("YG%d" % i, [128, 512], BF16) for i in range(2)]
        YST = [sb("YST0", [128, 4, 128], BF16)] * 2
        print("phase_S sbuf end", off[0], "limit 229376")
        pcbytes = (S_ * (L_ + 4) * 4 + 31) // 32 * 32
        PC2 = sb("PC2", [128, S_ * (L_ + 4)], F32, at=alias_base)
        ACC2 = sb("ACC2", [128, 1024], F32, at=alias_base + pcbytes)
        assert alias_base + pcbytes + 4096 <= alias_base + 2 * 1024 + 2 * 256 + 2048 + 2 * 2048
        ALIAS = [("XDTD", 0), ("XDTD", 1), ("Gd", 0), ("Gd", 1), "Lt", ("MT", 0), ("MT", 1)]
        PCs = [PC, PC2]
        ACCs = [ACC, ACC2]
        pc3s = [x[:, :].rearrange("p (s l) -> p s l", s=S_) for x in PCs]
        acc3s = [x[:, :].rearrange("p (s l) -> p s l", s=S_) for x in ACCs]
        PCk = [["PC"], ["PC2"] + ALIAS]
        ACCk = [["ACC"], ["ACC2"] + ALIAS]
        dma("sp", dtb_b[:], dtb.to_broadcast([128, 128]), writes=["dtb_b"])
        dma("sp", A_b[:], alog.to_broadcast([128, 128]), writes=["A_b"])
        dma("sp", dsk_b[:], dsk.to_broadcast([128, 64]), writes=["dsk_b"])
        dma("sp", cvw[:], convw, writes=["cvw"])
        dma("sp", cvb[:], convb, writes=["cvb"])
        dma("sp", nrw[:], normw, writes=["nrw"])
        P.op("dve", lambda e: e.memset(PC[:], 0.0), writes=["PC"])
        P.op("dve", lambda e: e.memset(PC2[:], 0.0), writes=PCk[1])
        act(A_b[:], A_b[:], AF.Exp, ["A_b"], ["A_b"])
        ts("dve", A_b[:], A_b[:], -1.0, None, ALU.mult, None, ["A_b"], ["A_b"])
        sdt, vdt = wload(wv[:, :, 10240:10368], [128, 16, 128])
        for t in range(NT):
            bk = t % 2
            b2 = 0
            for k in range(16):
                mm(ps[:, bk, 0:128], HT[:, k, t * 128:(t + 1) * 128], vdt[:, k, :], k == 0, k == 15, [("ring", sdt)], [("ps", bk)])
            tt("dve", DTt[:], ps[:, bk, 0:128], dtb_b[:], ALU.add, ["dtb_b"], [("ps", bk), "DTt"])
            act(DTt[:], DTt[:], AF.Exp, ["DTt"], ["DTt"])
            act(DTt[:], DTt[:], AF.Ln, ["DTt", "onesf"], ["DTt"], bias=onesf[:, 0:1], scale=1.0)
            tt("dve", DAf[b2][:], DTt[:], A_b[:], ALU.mult, ["DTt", "A_b"], [("DAf", b2)])
            cp("dve", DAb[:, t, :], DAf[b2][:], [("DAf", b2)], [("DAb", t)])
            mm(ps[:, 2, 0:64], tff[:], DAf[b2][:, 0:64], True, True, ["tff", ("DAf", b2)], [("ps", 2)])
            mm(ps[:, 2, 64:128], tbf[:], DAf[b2][:, 64:128], True, True, ["tbf", ("DAf", b2)], [("ps", 2)])
            mm(ps[:, 2, 128:256], onesf[:], DAf[b2][:], True, True, ["onesf", ("DAf", b2)], [("ps", 2)])
            ts("dve", NACS[:, 0, :], ps[:, 2, 0:128], -1.0, None, ALU.mult, None, [], [("ps", 2), ("NACS", 0)])
            cp("dve", TOTs[b2][:], ps[:, 2, 128:256], [], [("ps", 2), ("TOT", b2)])
            act(OFFD[:, t, :], NACS[:, 0, :], AF.Exp, [("NACS", 0)], [("OFFD", t)], scale=-1.0)
            act(CDEC[:, t, :], TOTs[b2][:], AF.Exp, [("TOT", b2)], [("CDEC", t)])
            tt("dve", WST[:, t, :], TOTs[b2][:], NACS[:, 0, :], ALU.add, [("TOT", b2), ("NACS", 0)], [("WST", t)])
            act(WST[:, t, :], WST[:, t, :], AF.Exp, [("WST", t)], [("WST", t)])
            tt("dve", WST[:, t, :], WST[:, t, :], DTt[:], ALU.mult, [("WST", t), "DTt"], [("WST", t)])
            act(DTt[:], DTt[:], AF.Ln, ["DTt", ("WST", t)], ["DTt"])
            cp("dve", LDH[:, t, :], DTt[:], ["DTt"], [("LDH", t)])
            tt("dve", DTt[:], DTt[:], LDH[:, t, :], ALU.subtract, ["DTt", ("LDH", t)], ["DTt"])
            cp("dve", LDL[:, t, :], DTt[:], ["DTt"], [("LDL", t)])
        pv = ps[:, 2, :].bitcast(BF16)
        pre_xs = [None]
        for g in range(8):
            cols = [(4096 + g * 512, 256), (4096 + g * 512 + 256, 256), (8192 + g * 128, 128), (9216 + g * 128, 128)]
            if pre_xs[0] is not None:
                wsl = pre_xs[0] + [wload(wv[:, :, c0:c0 + w], [128, 16, w]) for (c0, w) in cols[2:]]
                pre_xs[0] = None
            else:
                wsl = [wload(wv[:, :, c0:c0 + w], [128, 16, w]) for (c0, w) in cols]

            def inproj_mm(c):
                slot, v = wsl[c // 2] if c < 4 else wsl[c - 2]
                mo = (c % 2) * 128 if c < 4 else 0
                for n in range(2):
                    bk = n if c % 2 == 0 else 3 + n
                    for k in range(16):
                        mm(ps[:, bk, :], v[:, k, mo:mo + 128], HT[:, k, n * 512:(n + 1) * 512], k == 0, k == 15,
                           [("ring", slot)], [("ps", bk)])

            def inproj_evac(c):
                pc3 = pc3s[c % 2]
                if c % 2 == 1:
                    P.op("dve", lambda e: e.memset(pc3s[1][:, :, 0:2], 0.0), writes=PCk[1])
                    P.op("dve", lambda e: e.memset(pc3s[1][:, :, L_ + 2:L_ + 4], 0.0), writes=PCk[1])
                for n in range(2):
                    bk = n if c % 2 == 0 else 3 + n
                    if unit == 0:
                        cp("act", pc3[:, 2 * n:2 * n + 2, 2:L_ + 2], ps[:, bk, :].rearrange("p (s l) -> p s l", s=2),
                           [], [("ps", bk)] + PCk[c % 2])
                    else:
                        cp("act", pc3[:, 0, 2 + n * 512:2 + (n + 1) * 512], ps[:, bk, :], [], [("ps", bk)] + PCk[c % 2])

            inproj_mm(0)
            inproj_evac(0)
            for c in range(6):
                if c + 1 < 6:
                    inproj_mm(c + 1)
                ctile = (g * 4 + c) if c < 4 else (32 + g if c == 4 else 40 + g)
                pc3 = pc3s[c % 2]
                acc3 = acc3s[c % 2]
                pk = PCk[c % 2]
                ak = ACCk[c % 2]
                ts("dve", acc3, pc3[:, :, 0:L_], cvw[:, ctile, 0:1], cvb[:, ctile:ctile + 1], ALU.mult, ALU.add,
                   ["cvw", "cvb"] + (pk if c % 2 == 0 else []), ak + (pk if c % 2 == 1 else []))
                for kk in range(1, 5):
                    stt(acc3, pc3[:, :, kk:kk + L_], cvw[:, ctile, kk:kk + 1], acc3, ALU.mult, ALU.add,
                        ["cvw"] + (pk if c % 2 == 0 else []), ak + (pk if c % 2 == 1 else []))
                dstT = XCT1 if c < 4 else (BT if c == 4 else CT)
                dkey = "XCT1" if c < 4 else ("BT" if c == 4 else "CT")
                act(dstT[:], ACCs[c % 2][:], AF.Silu, ak if c % 2 == 0 else [], [dkey] + (ak if c % 2 == 1 else []))
                if c < 5:
                    for t in range(NT):
                        tr(pv[:, t * 128:(t + 1) * 128], dstT[:, t * 128:(t + 1) * 128], identb[:], [dkey, "identb"], [("ps", 2)])
                    src8 = pv[:, :].rearrange("p (t q) -> p t q", t=NT)
                    if c < 4:
                        cp("act", XS[:, :, c * 128:(c + 1) * 128], src8, [], [("ps", 2), "XS"])
                    else:
                        cp("act", Bt[:, :, :], src8, [], [("ps", 2), "Bt"])
                if c + 1 < 6:
                    inproj_evac(c + 1)
            wz = [wload(wv[:, :, g * 512 + hz * 256: g * 512 + (hz + 1) * 256], [128, 16, 256]) for hz in range(2)]
            if g + 1 < 8:
                pre_xs[0] = [wload(wv[:, :, c0:c0 + 256], [128, 16, 256])
                             for c0 in (4096 + (g + 1) * 512, 4096 + (g + 1) * 512 + 256)]

            iters = []
            for d in range(2):
                for sq in range(S_):
                    order = list(range(NCH)) if d == 0 else list(range(NCH - 1, -1, -1))
                    for ii, ch in enumerate(order):
                        iters.append((d, sq, ch, ii == 0, ii == NCH - 1))

            def front_a(i):
                d, sq, ch, first, last = iters[i]
                p = i % 2
                t = sq * NCH + ch
                T_d, neg_d, mask_d = (tfb, negf, maskf) if d == 0 else (tbb, negb, maskb)
                Tk, nk, mk = ("tfb", "negf", "maskf") if d == 0 else ("tbb", "negb", "maskb")
                nT_d, nTk = (ntfb, "ntfb") if d == 0 else (ntbb, "ntbb")
                c0 = d * 64 + g * 8
                wsv = WST[:, t, c0:c0 + 8].unsqueeze(2).to_broadcast([128, 8, 64])
                xs3 = XS[:, t, :].rearrange("p (h q) -> p h q", h=8)
                tt("pool", XDTD[p][:].rearrange("p (h q) -> p h q", h=8), xs3, wsv, ALU.mult, ["XS", ("WST", t)], [("XDTD", p)])
                mm(ps[:, 3, 0:128], BT[:, t * 128:(t + 1) * 128], CT[:, t * 128:(t + 1) * 128], True, True, ["BT", "CT"], [("ps", 3)])
                tt("dve", Gd[p][:], ps[:, 3, 0:128], mask_d[:], ALU.mult, [mk], [("ps", 3), ("Gd", p)])
                for hb in range(2):
                    bk = 4 + hb
                    for hh in range(4):
                        col = c0 + hb * 4 + hh
                        dab = DAb[:, t, col:col + 1].to_broadcast([128, 128])
                        mm(ps[:, bk, hh * 128:(hh + 1) * 128], identb[:], neg_d[:], True, False, ["identb", nk], [("ps", bk)])
                        mm(ps[:, bk, hh * 128:(hh + 1) * 128], dab, T_d[:], False, False, [("DAb", t), Tk], [("ps", bk)])
                        mm(ps[:, bk, hh * 128:(hh + 1) * 128], nT_d[:], dab, False, False, [("DAb", t), nTk], [("ps", bk)])
                        mm(ps[:, bk, hh * 128:(hh + 1) * 128], identb[:], LDH[:, t, col:col + 1].to_broadcast([128, 128]), False, False,
                           ["identb", ("LDH", t)], [("ps", bk)])
                        mm(ps[:, bk, hh * 128:(hh + 1) * 128], identb[:], LDL[:, t, col:col + 1].to_broadcast([128, 128]), False, True,
                           ["identb", ("LDL", t)], [("ps", bk)])
                for hb in range(2):
                    bk = 4 + hb
                    act(Lt[:, hb * 512:(hb + 1) * 512], ps[:, bk, :], AF.Exp, [], [("ps", bk), "Lt"])

            def front_b(i):
                p = i % 2
                tt("dve", MT[p][:].rearrange("p (h l) -> p h l", h=8), Lt[:].rearrange("p (h l) -> p h l", h=8),
                   Gd[p][:].unsqueeze(1).to_broadcast([128, 8, 128]), ALU.mult, ["Lt", ("Gd", p)], [("MT", p)])

            def back(i):
                d, sq, ch, first, last = iters[i]
                p = i % 2
                t = sq * NCH + ch
                c0 = d * 64 + g * 8
                xs3 = XS[:, t, :].rearrange("p (h q) -> p h q", h=8)
                zfirst = first and unit == 0
                if first:
                    if unit == 0:
                        pass
                    else:
                        r0 = d * 4096 + g * 512
                        dma("sp", STL, st_in[r0:r0 + 512, :].rearrange("(q p) n -> p q n", p=128), writes=["TMP"])
                        for q in range(4):
                            tr(ps[:, 2, q * 128:(q + 1) * 128], STL[:, q, :], identf[:], ["TMP", "identf"], [("ps", 2)])
                        cp("dve", ST[:], ps[:, 2, :], [], [("ps", 2), "ST"])
                        cp("dve", Sbf[i % 2][:], ST[:], ["ST"], [("Sbf", i % 2)])
                for hl in range(8):
                    mm(ps[:, 6, hl * 64:(hl + 1) * 64], MT[p][:, hl * 128:(hl + 1) * 128], XS[:, t, hl * 64:(hl + 1) * 64], True, True,
                       [("MT", p), "XS"], [("ps", 6)])
                if not zfirst:
                    mm(ps[:, 7, :], CT[:, t * 128:(t + 1) * 128], Sbf[i % 2][:], True, True, ["CT", ("Sbf", i % 2)], [("ps", 7)])
                mm(ps[:, 3, :], Bt[:, t, :], XDTD[p][:], True, True, ["Bt", ("XDTD", p)], [("ps", 3)])
                if zfirst:
                    cp("dve", ST[:], ps[:, 3, :], [], [("ps", 3), "ST"])
                else:
                    tt("dve", ST[:].rearrange("p (h q) -> p h q", h=8), ST[:].rearrange("p (h q) -> p h q", h=8),
                       CDEC[:, t, c0:c0 + 8].unsqueeze(2).to_broadcast([128, 8, 64]), ALU.mult, ["ST", ("CDEC", t)], ["ST"])
                    tt("dve", ST[:], ST[:], ps[:, 3, :], ALU.add, ["ST"], [("ps", 3), "ST"])
                if not last:
                    cp("act", Sbf[(i + 1) % 2][:], ST[:], ["ST"], [("Sbf", (i + 1) % 2)])
                if not zfirst:
                    tt("dve", TMP[:].rearrange("p (h q) -> p h q", h=8), ps[:, 7, :].rearrange("p (h q) -> p h q", h=8),
                       OFFD[:, t, c0:c0 + 8].unsqueeze(2).to_broadcast([128, 8, 64]), ALU.mult, [("OFFD", t)], [("ps", 7), "TMP"])
                ya = YA[p]
                yk = ("YA", p)
                if zfirst:
                    cp("dve", ya[:], ps[:, 6, :], [], [("ps", 6), yk])
                else:
                    tt("dve", ya[:], ps[:, 6, :], TMP[:], ALU.add, ["TMP"], [("ps", 6), yk])
                ydst_ = yfs if d == 0 else yfs2
                dma("sp", ydst_[t * 128:(t + 1) * 128, :], ya[:], [yk], [("yfs", d, t)])
                if last and unit == 0:
                    for q in range(4):
                        tr(ps[:, 2, q * 128:(q + 1) * 128], ST[:, q * 128:(q + 1) * 128], identf[:], ["ST", "identf"], [("ps", 2)])
                    cp("act", TMP[:], ps[:, 2, :], [], [("ps", 2), "TMP"])
                    r0 = sq * 8192 + d * 4096 + g * 512
                    dma("sp", so[r0:r0 + 512, :].rearrange("(q p) n -> p q n", p=128), STL, ["TMP"], [("so", sq, d, g)])

            for i in range(len(iters)):
                front_a(i)
                if i > 0:
                    back(i - 1)
                front_b(i)
            back(len(iters) - 1)
            def zproj(t):
                bkz = t % 2
                for hz in range(2):
                    slot, v = wz[hz]
                    for k in range(16):
                        mm(ps[:, bkz, hz * 256:(hz + 1) * 256], HT[:, k, t * 128:(t + 1) * 128], v[:, k, :], k == 0, k == 15,
                           [("ring", slot)], [("ps", bkz)])

            zproj(0)
            for t0 in range(2):
                dma("sp", YA[t0][:], yfs[t0 * 128:(t0 + 1) * 128, :], [("yfs", 0, t0)], [("YA", t0)])
                dma("sp", YFLs[t0][:], yfs2[t0 * 128:(t0 + 1) * 128, :], [("yfs", 1, t0)], [YFLk[t0]])
            for t in range(NT):
                p = t % 2
                ya = YA[p]
                yk = ("YA", p)
                bkz = t % 2
                act(SZ[p][:], ps[:, bkz, :], AF.Silu, [], [("ps", bkz), ("SZ", p)])
                if t + 1 < NT:
                    zproj(t + 1)
                tt("pool", TMP[:].rearrange("p (h q) -> p h q", h=8), XS[:, t, :].rearrange("p (h q) -> p h q", h=8),
                   dsk_b[:, g * 8:(g + 1) * 8].unsqueeze(2).to_broadcast([128, 8, 64]), ALU.mult, ["XS", "dsk_b"], ["TMP"])
                tt("dve", ya[:], ya[:], YFLs[p][:], ALU.add, [yk, YFLk[p]], [yk])
                tt("dve", ya[:], ya[:], TMP[:], ALU.add, [yk, "TMP"], [yk])
                tt("dve", YG[p][:], ya[:], SZ[p][:], ALU.mult, [yk, ("SZ", p)], [("YG", p)])
                if t + 2 < NT:
                    dma("sp", ya[:], yfs[(t + 2) * 128:(t + 3) * 128, :], [("yfs", 0, t + 2)], [yk])
                    dma("sp", YFLs[p][:], yfs2[(t + 2) * 128:(t + 3) * 128, :], [("yfs", 1, t + 2)], [YFLk[p]])
                P.op("act", lambda e, t=t, g=g, p=p: e.activation(out=SZ[p][:], in_=YG[p][:], func=AF.Square, accum_out=SSQ[:, t, g:g + 1]),
                     [("YG", p)], [("SZ", p), ("SSQ", t, g)])
                bkt = 6 + p
                pvt = ps[:, bkt, :].bitcast(BF16)
                for c in range(4):
                    tr(pvt[:, c * 128:(c + 1) * 128], YG[p][:, c * 128:(c + 1) * 128], identb[:], [("YG", p), "identb"], [("ps", bkt)])
                for c in range(4):
                    act(YST[p][:, c, :], pvt[:, c * 128:(c + 1) * 128], AF.Copy, ["nrw"], [("ps", bkt), ("YST", 0)],
                        scale=nrw[:, g * 4 + c:g * 4 + c + 1])
                dma("sp", ysc.rearrange("(c p) t -> p c t", p=128)[:, g * 4:(g + 1) * 4, t * 128:(t + 1) * 128], YST[p][:],
                    [("YST", 0)], [("ysc", g, t)])
        P.op("dve", lambda e: e.reduce_sum(out=RMSR[:], in_=SSQ[:], axis=mybir.AxisListType.X),
             [("SSQ", t, g) for t in range(NT) for g in range(8)], ["rmsr"])
        act(RMSR[:], RMSR[:], AF.Sqrt, ["rmsr", "epst"], ["rmsr"], bias=epst[:, 0:1], scale=1.0 / 4096.0)
        P.op("dve", lambda e: e.reciprocal(out=RMSR[:], in_=RMSR[:]), ["rmsr"], ["rmsr"])
        P.barrier()

    STAGE = int(os.environ.get("MK_STAGE", "99"))
    if MKM >= 4:
        P.barrier()
    for unit in range(2):
        phase_L(0, unit)
        phase_F(unit)
        phase_O(0, unit, fno_w_out, None)
        if DEPTH_RUN >= 2:
            phase_L(1, unit)
            phase_S(unit)
            phase_O(1, unit, ssd_w_out, RMSR)

    build_and_emit(nc, P)
    return nc


def _pos_table():
    def sincos(p, dim):
        omega = 1.0 / (10000.0 ** (np.arange(dim // 2, dtype=np.float32) / np.float32(dim / 2)))
        ang = p.astype(np.float32)[:, None] * omega[None, :].astype(np.float32)
        return np.concatenate([np.sin(ang), np.cos(ang)], axis=-1).astype(np.float32)
    t = np.arange(1024)
    row, col = t // 64, t % 64
    return np.concatenate([sincos(row, 1024), sincos(col, 1024)], axis=-1).astype(np.float32)


def _consts():
    bf = ml_dtypes.bfloat16
    i = np.arange(128)
    c = {}
    c["c_identb"] = np.eye(128, dtype=np.float32).astype(bf)
    c["c_identf"] = np.eye(128, dtype=np.float32)
    tf = (i[:, None] <= i[None, :]).astype(np.float32)
    tb = (i[:, None] >= i[None, :]).astype(np.float32)
    c["c_tfb"] = tf.astype(bf)
    c["c_tbb"] = tb.astype(bf)
    c["c_tff"] = tf
    c["c_tbf"] = tb
    NEG = -30000.0
    c["c_negf"] = (NEG * (i[None, :] < i[:, None])).astype(np.float32).astype(bf)
    c["c_negb"] = (NEG * (i[None, :] > i[:, None])).astype(np.float32).astype(bf)
    c["c_maskf"] = (i[None, :] >= i[:, None]).astype(np.float32).astype(bf)
    c["c_maskb"] = (i[None, :] <= i[:, None]).astype(np.float32).astype(bf)
    c["c_onesf"] = np.ones((128, 128), np.float32)
    c["c_ntfb"] = (-tf).astype(bf)
    c["c_ntbb"] = (-tb).astype(bf)
    j = np.arange(256)
    ang = 2.0 * np.pi * ((j[:, None] * j[None, :]) % 256) / 256.0
    cs = np.concatenate([np.cos(ang), np.sin(ang)], axis=1) / 16.0
    c["c_csc"] = cs.reshape(2, 128, 512).transpose(1, 0, 2).astype(np.float32).astype(bf)
    cl = np.cos(ang) / 16.0
    sl = -np.sin(ang) / 16.0
    d256 = np.stack([cl, sl], axis=0)
    c["c_dft256"] = d256.reshape(2, 2, 128, 256).transpose(2, 1, 0, 3).astype(np.float32).astype(bf)
    j = np.arange(1024)
    ang = 2.0 * np.pi * ((j[:, None] * j[None, :]) % 1024) / 1024.0
    c["c_dft1024"] = np.stack([np.cos(ang) / 32.0, -np.sin(ang) / 32.0], axis=0).astype(np.float32).astype(bf)
    return c


_NC_CACHE = {}


def kernel(x_prompt, x_sample, state_ssd_ctx, c, c_ctx, w_ada, b_ada, ln_g, ln_b, fno_w_in, fno_w_out,
           ssd_w_in, ssd_conv_w, ssd_conv_b, ssd_dt_bias, ssd_a_log, ssd_d, ssd_norm_w, ssd_w_out):
    f = lambda a: np.ascontiguousarray(np.asarray(a, dtype=np.float32))
    x_prompt, x_sample, state_ssd_ctx, c, c_ctx = map(f, (x_prompt, x_sample, state_ssd_ctx, c, c_ctx))
    if "nc" not in _NC_CACHE:
        _NC_CACHE["nc"] = build_nc()
        _NC_CACHE["consts"] = _consts()
        _NC_CACHE["pos"] = _pos_table()
    nc = _NC_CACHE["nc"]
    consts = _NC_CACHE["consts"]
    shared = dict(consts)
    shared["pos"] = _NC_CACHE["pos"]
    shared["w_ada"] = f(w_ada)
    shared["bT"] = f(np.asarray(b_ada).reshape(2, 48, 128).transpose(2, 0, 1))
    shared["ln_g"] = f(ln_g)
    shared["ln_b"] = f(ln_b)
    shared["fno_w_in"] = f(fno_w_in[0])
    shared["fno_w_out"] = f(fno_w_out[0])
    shared["ssd_w_in"] = f(ssd_w_in[0])
    shared["ssd_w_out"] = f(ssd_w_out[0])
    shared["convw"] = f(np.asarray(ssd_conv_w[0]).reshape(5, 48, 128).transpose(2, 1, 0))
    shared["convb"] = f(np.asarray(ssd_conv_b[0]).reshape(48, 128).T)
    shared["dtb"] = f(np.asarray(ssd_dt_bias[0]).reshape(1, 128))
    shared["alog"] = f(np.asarray(ssd_a_log[0]).reshape(1, 128))
    shared["dsk"] = f(np.asarray(ssd_d[0]).reshape(1, 64))
    shared["normw"] = f(np.asarray(ssd_norm_w[0]).reshape(32, 128).T)
    in_maps = []
    import os
    NCORE = int(os.environ.get("MK_CORES", "8"))
    for core in range(NCORE):
        m = dict(shared)
        m["xa"] = x_prompt[4 * core:4 * core + 4].reshape(1024, D)
        m["xb"] = x_sample[core]
        m["st_in"] = state_ssd_ctx[core, 0].reshape(2 * 64 * 64, 128)
        cond = np.stack([c_ctx, c[core]], axis=0)
        m["cT"] = f(cond.reshape(2, 16, 128).transpose(2, 1, 0))
        in_maps.append(m)
    res = run_bass_kernel_spmd(nc, in_maps, core_ids=list(range(NCORE)))
    r = res.results
    y_prompt = np.concatenate([r[i]["ya"].reshape(4, 256, D) for i in range(NCORE)], axis=0)
    y_sample = np.stack([r[i]["yb"] for i in range(NCORE)], axis=0)
    state = np.concatenate([r[i]["so"].reshape(4, 1, 2, 64, 64, 128) for i in range(NCORE)], axis=0)
    return (y_prompt.astype(np.float32), y_sample.astype(np.float32), state.astype(np.float32))
```

```python
import math
from contextlib import ExitStack

import numpy as np
import ml_dtypes

import concourse.bass as bass
import concourse.mybir as mybir
from concourse.bass_utils import run_bass_kernel_spmd

F32 = mybir.dt.float32
BF16 = mybir.dt.bfloat16
ALU = mybir.AluOpType
AF = mybir.ActivationFunctionType

ENGS = ("pe", "act", "dve", "pool", "sp")
DMA_POOL = {"sp": 14, "act": 4, "pool": 10}

D = 2048
NT = 8
ALPHA = 4.0 ** 0.25
EPS = 1e-5
DEPTH_RUN = 2


class _Op:
    __slots__ = ("eng", "fn", "deps", "dma", "needed", "val", "sem", "prev_same_sem")

    def __init__(self, eng, fn, dma):
        self.eng = eng
        self.fn = fn
        self.deps = []
        self.dma = dma
        self.needed = False
        self.val = None
        self.sem = None
        self.prev_same_sem = None


class Prog:
    def __init__(self, nc):
        self.nc = nc
        self.ops = {e: [] for e in ENGS}
        self.last_w = {}
        self.readers = {}
        self.dma_count = {e: 0 for e in DMA_POOL}
        self.dma_last = {}
        self.last_compute = {}
        self.bar = None
        self.bar_seen = set()

    def op(self, eng, fn, reads=(), writes=(), dma=False):
        o = _Op(eng, fn, dma)
        deps = {}
        for k in reads:
            w = self.last_w.get(k)
            if w is not None:
                deps[id(w)] = w
        for k in writes:
            w = self.last_w.get(k)
            if w is not None:
                deps[id(w)] = w
            for r in self.readers.get(k, {}).values():
                deps[id(r)] = r
        if self.bar is not None and eng not in self.bar_seen:
            self.bar_seen.add(eng)
            deps[id(self.bar)] = self.bar
        for d in deps.values():
            if d is o:
                continue
            if d.eng == "pe" and eng == "pe" and not d.dma and not dma:
                continue
            d.needed = True
            o.deps.append(d)
        if dma:
            n = self.dma_count[eng]
            self.dma_count[eng] = n + 1
            slot = (eng, n % DMA_POOL[eng])
            o.sem = slot
            o.prev_same_sem = self.dma_last.get(slot)
            o.val = 16 * (n // DMA_POOL[eng] + 1)
            self.dma_last[slot] = o
            o.needed = True
            rk = slot
        else:
            self.last_compute[eng] = o
            rk = eng
        for k in writes:
            self.last_w[k] = o
            self.readers[k] = {}
        for k in reads:
            if k not in writes:
                self.readers.setdefault(k, {})[rk] = o
        self.ops[eng].append(o)
        return o

    def barrier(self):
        b = _Op("sp", lambda e: e.nop(), False)
        deps = list(self.last_compute.values()) + list(self.dma_last.values())
        for d in deps:
            d.needed = True
            b.deps.append(d)
        b.needed = True
        self.ops["sp"].append(b)
        self.last_compute["sp"] = b
        self.bar = b
        self.bar_seen = {"sp"}
        self.last_w.clear()
        self.readers.clear()

    def emit(self, sems, dsems):
        for e in ENGS:
            c = 0
            for o in self.ops[e]:
                if o.dma:
                    continue
                if o.needed:
                    c += 1
                    o.val = c
                    o.sem = e
        final_waits = [(o.sem, o.val) for o in self.dma_last.values()]

        def run(e, eng):
            waited = {}

            def wait(semkey, val):
                if waited.get(semkey, 0) >= val:
                    return
                waited[semkey] = val
                s = dsems[semkey] if isinstance(semkey, tuple) else sems[semkey]
                eng.wait_ge(s, val)

            for o in self.ops[e]:
                need = {}
                for d in o.deps:
                    if need.get(d.sem, 0) < d.val:
                        need[d.sem] = d.val
                if o.dma and o.prev_same_sem is not None:
                    p = o.prev_same_sem
                    if need.get(p.sem, 0) < p.val:
                        need[p.sem] = p.val
                for k, v in need.items():
                    wait(k, v)
                ins = o.fn(eng)
                if o.dma:
                    ins.then_inc(dsems[o.sem], 16)
                elif o.needed:
                    ins.then_inc(sems[e], 1)
            if e == "sp":
                for k, v in final_waits:
                    wait(k, v)

        return run


def build_and_emit(nc, prog):
    with ExitStack() as st:
        sems = {e: st.enter_context(nc.semaphore("s_" + e)) for e in ENGS}
        dsems = {}
        for e, n in DMA_POOL.items():
            for i in range(n):
                dsems[(e, i)] = st.enter_context(nc.semaphore("d_%s_%d" % (e, i)))
        run = prog.emit(sems, dsems)
        block = st.enter_context(nc.Block())

        @block.tensor
        def _(eng):
            run("pe", eng)

        @block.scalar
        def _(eng):
            run("act", eng)

        @block.vector
        def _(eng):
            run("dve", eng)

        @block.gpsimd
        def _(eng):
            run("pool", eng)

        @block.sync
        def _(eng):
            run("sp", eng)


def build_nc():
    nc = bass.Bass("TRN2", target_bir_lowering=False)
    P = Prog(nc)

    def din(name, shape, dt=F32):
        return nc.dram_tensor(name, shape, dt, kind="ExternalInput").ap()

    def dout(name, shape, dt=F32):
        return nc.dram_tensor(name, shape, dt, kind="ExternalOutput").ap()

    xa = din("xa", [1024, D])
    xb = din("xb", [1024, D])
    pos = din("pos", [1024, D])
    st_in = din("st_in", [2 * 64 * 64, 128])
    cT = din("cT", [128, 16, 2])
    bT = din("bT", [128, 2, 48])
    w_ada = din("w_ada", [2, D, 3 * D])
    ln_g = din("ln_g", [2, D])
    ln_b = din("ln_b", [2, D])
    fno_w_in = din("fno_w_in", [D, 4 * D])
    fno_w_out = din("fno_w_out", [2 * D, D])
    ssd_w_in = din("ssd_w_in", [D, 10368])
    ssd_w_out = din("ssd_w_out", [2 * D, D])
    convw = din("convw", [128, 48, 5])
    convb = din("convb", [128, 48])
    dtb = din("dtb", [1, 128])
    alog = din("alog", [1, 128])
    dsk = din("dsk", [1, 64])
    normw = din("normw", [128, 32])
    c_identb = din("c_identb", [128, 128], BF16)
    c_identf = din("c_identf", [128, 128])
    c_tfb = din("c_tfb", [128, 128], BF16)
    c_tbb = din("c_tbb", [128, 128], BF16)
    c_tff = din("c_tff", [128, 128])
    c_tbf = din("c_tbf", [128, 128])
    c_negf = din("c_negf", [128, 128], BF16)
    c_negb = din("c_negb", [128, 128], BF16)
    c_maskf = din("c_maskf", [128, 128], BF16)
    c_maskb = din("c_maskb", [128, 128], BF16)
    c_onesf = din("c_onesf", [128, 128])
    c_ntfb = din("c_ntfb", [128, 128], BF16)
    c_ntbb = din("c_ntbb", [128, 128], BF16)
    c_csc = din("c_csc", [128, 2, 512], BF16)
    c_dft256 = din("c_dft256", [128, 2, 2, 256], BF16)
    c_dft1024 = din("c_dft1024", [2, 1024, 1024], BF16)

    ya = dout("ya", [1024, D])
    yb = dout("yb", [1024, D])
    so = dout("so", [4 * 2 * 64 * 64, 128])
    ysc = nc.dram_tensor("ysc", [4096, 1024], BF16, kind="Internal").ap()
    yfs = nc.dram_tensor("yfs", [1024, 512], F32, kind="Internal").ap()
    yfs2 = nc.dram_tensor("yfs2", [1024, 512], F32, kind="Internal").ap()

    off = [16512]

    def sb(name, shape, dt, at=None):
        nbytes = int(np.prod(shape[1:])) * (4 if dt == F32 else 2)
        nbytes = (nbytes + 31) // 32 * 32
        if at is None:
            o = off[0]
            off[0] += nbytes
        else:
            o = at
        assert o + nbytes <= 229376, (name, o, nbytes)
        return nc.alloc_sbuf_tensor_at(name, list(shape), dt, offset=o)

    X1 = sb("X1", [128, NT, D], F32)
    HT = sb("HT", [128, 16, 1024], BF16)
    RING = [sb("ring%d" % i, [128, 4096], BF16) for i in range(4)]
    identb = sb("identb", [128, 128], BF16)
    identf = sb("identf", [128, 128], F32)
    tfb = sb("tfb", [128, 128], BF16)
    tbb = sb("tbb", [128, 128], BF16)
    tff = sb("tff", [128, 128], F32)
    tbf = sb("tbf", [128, 128], F32)
    negf = sb("negf", [128, 128], BF16)
    negb = sb("negb", [128, 128], BF16)
    maskf = sb("maskf", [128, 128], BF16)
    maskb = sb("maskb", [128, 128], BF16)
    onesf = sb("onesf", [128, 128], F32)
    ntfb = sb("ntfb", [128, 128], BF16)
    ntbb = sb("ntbb", [128, 128], BF16)
    modT = sb("modT", [128, 2, 48, 2], F32)
    epst = sb("epst", [128, 1], F32)
    _save = off[0]
    off[0] = 228352
    cTs = sb("cTs", [128, 16, 2], F32)
    scT = sb("scT", [128, 16, 2], BF16)
    bTs = sb("bTs", [128, 2, 48], F32)
    off[0] = _save
    PH = None

    def phase_alloc():
        off[0] = PH2

    ps = nc.alloc_psum_tensor("ps", [128, 8, 512], F32)

    def dma(q, out, in_, reads=(), writes=()):
        return P.op(q, lambda e: e.dma_start(out=out, in_=in_), reads, writes, dma=True)

    def mm(out, lhsT, rhs, start, stop, reads, writes):
        return P.op("pe", lambda e: e.matmul(out, lhsT=lhsT, rhs=rhs, start=start, stop=stop), reads, writes)

    def tr(out, in_, ident, reads, writes):
        return P.op("pe", lambda e: e.transpose(out, in_, ident), reads, writes)

    def act(out, in_, func, reads, writes, bias=None, scale=None):
        kw = {}
        if bias is not None:
            kw["bias"] = bias
        if scale is not None:
            kw["scale"] = scale
        return P.op("act", lambda e: e.activation(out=out, in_=in_, func=func, **kw), reads, writes)

    def tt(eng, out, in0, in1, op, reads, writes):
        return P.op(eng, lambda e: e.tensor_tensor(out=out, in0=in0, in1=in1, op=op), reads, writes)

    def ts(eng, out, in0, s1, s2, op0, op1, reads, writes):
        if s2 is None:
            return P.op(eng, lambda e: e.tensor_scalar(out=out, in0=in0, scalar1=s1, scalar2=None, op0=op0), reads, writes)
        return P.op(eng, lambda e: e.tensor_scalar(out=out, in0=in0, scalar1=s1, scalar2=s2, op0=op0, op1=op1), reads, writes)

    def stt(out, in0, scalar, in1, op0, op1, reads, writes):
        return P.op("dve", lambda e: e.scalar_tensor_tensor(out=out, in0=in0, scalar=scalar, in1=in1, op0=op0, op1=op1),
                    reads, writes)

    def cp(eng, out, in_, reads, writes):
        if eng == "act":
            return P.op("act", lambda e: e.copy(out=out, in_=in_), reads, writes)
        return P.op(eng, lambda e: e.tensor_copy(out=out, in_=in_), reads, writes)

    ring_ctr = [0]

    def ring_next():
        i = ring_ctr[0] % 4
        ring_ctr[0] += 1
        return i

    bank_ctr = [0]

    def bank_next(n=8):
        b = bank_ctr[0] % n
        bank_ctr[0] += 1
        return b

    def wload(src_ap, view_shape, cast=True):
        i = ring_next()
        n = int(np.prod(view_shape[1:]))
        v = RING[i][:, 0:n]
        if len(view_shape) == 3:
            v = v.rearrange("p (a b) -> p a b", a=view_shape[1])
        dma("pool" if cast else "sp", v, src_ap, writes=[("ring", i)])
        return i, v

    for t, s, k in ((identb, c_identb, "identb"), (identf, c_identf, "identf"), (tfb, c_tfb, "tfb"), (tbb, c_tbb, "tbb"),
                    (tff, c_tff, "tff"), (tbf, c_tbf, "tbf"), (negf, c_negf, "negf"), (negb, c_negb, "negb"),
                    (maskf, c_maskf, "maskf"), (maskb, c_maskb, "maskb"), (onesf, c_onesf, "onesf"), (ntfb, c_ntfb, "ntfb"), (ntbb, c_ntbb, "ntbb"),
                    (cTs, cT, "cTs"), (bTs, bT, "bTs")):
        dma("sp", t[:], s, writes=[k])
    CONSTS = ["identb", "identf", "tfb", "tbb", "tff", "tbf", "negf", "negb", "maskf", "maskb", "onesf", "csc",
              "dft256", "modT", "epst"]
    P.op("dve", lambda e: e.memset(epst[:], EPS), writes=["epst"])

    import os
    MKM = int(os.environ.get("MK_M", "99"))
    if MKM >= 1:
        act(scT[:], cTs[:], AF.Silu, ["cTs"], ["scT"])
    def m_slab(i, sl, bank):
        wv = w_ada[i].rearrange("(k p) n -> p k n", p=128)
        slot, v = wload(wv[:, :, sl * 256:(sl + 1) * 256], [128, 16, 256])
        for mmi in range(2):
            m = 2 * sl + mmi
            for k in range(16):
                mm(ps[:, bank, 2 * m:2 * m + 2], v[:, k, mmi * 128:(mmi + 1) * 128], scT[:, k, :], k == 0, k == 15,
                   [("ring", slot), "scT"], [("ps", bank)])

    def m_final(i, bank):
        tt("dve", modT[:, i, :, :], ps[:, bank, 0:96].rearrange("p (m c) -> p m c", c=2),
           bTs[:, i, :].unsqueeze(2).to_broadcast([128, 48, 2]), ALU.add, ["bTs"], [("ps", bank), "modT"])
        ts("dve", modT[:, i, 16:32, :], modT[:, i, 16:32, :], 1.0, None, ALU.add, None, ["modT"], ["modT"])

    for sl in range(24):
        m_slab(0, sl, 0)
    m_final(0, 0)

    def phase_L(layer, unit):
        phase_alloc()
        ci = unit
        xsrc = xa if unit == 0 else xb
        xn = [sb("xn%d" % i, [128, D], BF16) for i in range(2)]
        pt = [sb("pt%d" % i, [128, D], F32) for i in range(2)] if (layer == 0 and unit == 1) else None
        stat_tiles = []
        for i in range(2):
            stat_tiles.append(dict(
                st6=sb("Lst6%d" % i, [128, 4, 6], F32), mv=sb("Lmv%d" % i, [128, 2], F32),
                sd=sb("Lsd%d" % i, [128, 1], F32), rstd=sb("Lrstd%d" % i, [128, 1], F32),
                nmr=sb("Lnmr%d" % i, [128, 1], F32)))
        def L_prep(t):
            b = t % 2
            xk = ("X1", t)
            x_ap = X1[:, t, :]
            if layer == 0:
                dma("sp", x_ap, xsrc[t * 128:(t + 1) * 128, :], writes=[xk])
                if unit == 1:
                    dma("sp", pt[b][:], pos[t * 128:(t + 1) * 128, :], writes=[("pt", b)])
                    tt("dve", x_ap, x_ap, pt[b][:], ALU.add, [xk, ("pt", b)], [xk])
            S = stat_tiles[b]
            kk = ("Lst", b)
            for c in range(4):
                P.op("dve", lambda e, c=c, S=S, x_ap=x_ap: e.bn_stats(out=S["st6"][:, c, :], in_=x_ap[:, c * 512:(c + 1) * 512]),
                     [xk], [(kk, "st6")])
            P.op("dve", lambda e, S=S: e.bn_aggr(out=S["mv"][:], in_=S["st6"][:].rearrange("p a b -> p (a b)")),
                 [(kk, "st6")], [(kk, "mv")])
            act(S["sd"][:], S["mv"][:, 1:2], AF.Sqrt, [(kk, "mv"), "epst"], [(kk, "sd")], bias=epst[:, 0:1], scale=1.0)
            P.op("dve", lambda e, S=S: e.reciprocal(out=S["rstd"][:], in_=S["sd"][:]), [(kk, "sd")], [(kk, "rstd")])
            stt(S["nmr"][:], S["mv"][:, 0:1], -1.0, S["rstd"][:], ALU.mult, ALU.mult, [(kk, "mv"), (kk, "rstd")], [(kk, "nmr")])
            act(xn[b][:], x_ap, AF.Identity, [xk, (kk, "rstd"), (kk, "nmr")], [("xn", b)],
                bias=S["nmr"][:, 0:1], scale=S["rstd"][:, 0:1])
            for kc in range(16):
                bk = 4 + 2 * (t % 2) + (kc // 8)
                pv = ps[:, bk, :].bitcast(BF16)
                src = pv[:, (kc % 8) * 128:(kc % 8 + 1) * 128]
                tr(src, xn[b][:, kc * 128:(kc + 1) * 128], identb[:], [("xn", b), "identb"], [("ps", bk)])

        def L_evac(t):
            b = t % 2
            for kc in range(16):
                bk = 4 + 2 * (t % 2) + (kc // 8)
                pv = ps[:, bk, :].bitcast(BF16)
                src = pv[:, (kc % 8) * 128:(kc % 8 + 1) * 128]
                dst = HT[:, kc, t * 128:(t + 1) * 128]
                sc = modT[:, layer, 16 + kc, ci:ci + 1]
                sh = modT[:, layer, kc, ci:ci + 1]
                if kc < 8:
                    ts("dve", dst, src, sc, sh, ALU.mult, ALU.add, ["modT"], [("ps", bk), ("HT", t, kc)])
                else:
                    act(dst, src, AF.Identity, ["modT"], [("ps", bk), ("HT", t, kc)], bias=sh, scale=sc)

        L_prep(0)
        for t in range(NT):
            if t + 1 < NT:
                L_prep(t + 1)
            L_evac(t)
        P.barrier()

    HT_ALL = [("HT", t) for t in range(NT)]

    def phase_F(unit):
        phase_alloc()
        UT = [sb("UT%d" % i, [128, 2, 1024], BF16) for i in range(2)]
        ZT = [sb("ZT%d" % i, [128, 2, 1024], BF16) for i in range(2)]
        AA = [sb("AA%d" % i, [128, NT, 512], BF16) for i in range(2)]
        YS = [sb("YS%d" % i, [128, 512], BF16) for i in range(4)]
        csc = sb("csc", [128, 2, 512], BF16)
        dft256 = sb("dft256", [128, 2, 2, 256], BF16)
        dma("sp", csc[:], c_csc, writes=["csc"])
        dma("sp", dft256[:], c_dft256, writes=["dft256"])
        ys_ctr = 0
        wv = fno_w_in.rearrange("(k p) n -> p k n", p=128)
        dfv = c_dft1024.rearrange("c (k p) q -> c p k q", p=128)
        nb_F = 7 if unit == 0 else 8
        for g in range(16):
            gb = g % 2
            if unit == 0 and g >= 1 and g <= 12:
                m_slab(1, 2 * (g - 1), 7)
                m_slab(1, 2 * (g - 1) + 1, 7)
                if g == 12:
                    m_final(1, 7)
            su, vu = wload(wv[:, :, g * 256:(g + 1) * 256], [128, 16, 256])
            sz, vz = wload(wv[:, :, 4096 + g * 256:4096 + (g + 1) * 256], [128, 16, 256])
            for which, slot, v in (("u", su, vu), ("z", sz, vz)):
                for m in range(2):
                    for n in range(2):
                        bk = bank_next(nb_F)
                        for k in range(16):
                            mm(ps[:, bk, :], v[:, k, m * 128:(m + 1) * 128], HT[:, k, n * 512:(n + 1) * 512], k == 0, k == 15,
                               [("ring", slot)] + [("HT", t_, k) for t_ in range(n * 4, n * 4 + 4)], [("ps", bk)])
                        if which == "u":
                            cp("act", UT[gb][:, m, n * 512:(n + 1) * 512], ps[:, bk, :], [], [("ps", bk), ("UT", gb, m, n)])
                        else:
                            act(ZT[gb][:, m, n * 512:(n + 1) * 512], ps[:, bk, :], AF.Silu, [], [("ps", bk), ("ZT", gb, m, n)])
            for t in range(NT):
                bk = bank_next(nb_F)
                n = t // 4
                for m in range(2):
                    mm(ps[:, bk, :], UT[gb][:, m, t * 128:(t + 1) * 128], csc[:, m, :], m == 0, m == 1,
                       [("UT", gb, m, n), "csc"], [("ps", bk)])
                cp("dve", AA[gb][:, t, :], ps[:, bk, :], [], [("ps", bk), ("AA", gb, t)])
            if unit == 1:
                slabs = {}
                for n in range(2):
                    for c in range(2):
                        slabs[(n, c)] = wload(dfv[c][:, :, n * 512:(n + 1) * 512], [128, 8, 512], cast=False)
            for n in range(2):
                for m in range(2):
                    bk = bank_next(nb_F)
                    if unit == 1:
                        idx = 0
                        for kp in range(8):
                            for c in range(2):
                                slot, v = slabs[(n, c)]
                                mm(ps[:, bk, :], AA[gb][:, kp, c * 256 + m * 128: c * 256 + (m + 1) * 128], v[:, kp, :],
                                   idx == 0, idx == 15, [("AA", gb, kp), ("ring", slot)], [("ps", bk)])
                                idx += 1
                    else:
                        for sq in range(2):
                            s = n * 2 + sq
                            idx = 0
                            for kp in range(2):
                                t = s * 2 + kp
                                for c in range(2):
                                    mm(ps[:, bk, sq * 256:(sq + 1) * 256],
                                       AA[gb][:, t, c * 256 + m * 128: c * 256 + (m + 1) * 128], dft256[:, kp, c, :],
                                       idx == 0, idx == 3, [("AA", gb, t), "dft256"], [("ps", bk)])
                                    idx += 1
                    yi = ys_ctr % 4
                    ys_ctr += 1
                    tt("dve", YS[yi][:], ps[:, bk, :], ZT[gb][:, m, n * 512:(n + 1) * 512], ALU.mult,
                       [("ZT", gb, m, n)], [("ps", bk), ("YS", yi)])
                    kidx = g * 2 + m
                    dma("sp", ysc[kidx * 128:(kidx + 1) * 128, n * 512:(n + 1) * 512], YS[yi][:],
                        [("YS", yi)], [("ysc", kidx, n)])
        P.barrier()

    def phase_O(layer, unit, w_out, rstd_tile):
        phase_alloc()
        ci = unit
        YTH = sb("YTH", [128, 16, 1024], BF16)
        gate_b = sb("gate_b", [128, D], F32)
        tmp = [sb("otmp%d" % i, [128, 512], F32) for i in range(2)]
        GB = sb("GB", [128, D], F32)
        BB = sb("BB", [128, D], F32)
        dg = [sb("dg%d" % i, [128, 128], F32) for i in range(2)]
        stat_tiles = []
        for i in range(2):
            stat_tiles.append(dict(
                st6=sb("Ost6%d" % i, [128, 4, 6], F32), mv=sb("Omv%d" % i, [128, 2], F32),
                sd=sb("Osd%d" % i, [128, 1], F32), rstd=sb("Orstd%d" % i, [128, 1], F32),
                nmr=sb("Onmr%d" % i, [128, 1], F32)))
        dma("sp", GB[:], ln_g[layer:layer + 1, :].to_broadcast([128, D]), writes=["GB"])
        dma("sp", BB[:], ln_b[layer:layer + 1, :].to_broadcast([128, D]), writes=["BB"])
        for kc in range(16):
            d = dg[kc % 2]
            ts("dve", d[:], identf[:], modT[:, layer, 32 + kc, ci:ci + 1], None, ALU.mult, None, ["identf", "modT"], [("dg", kc % 2)])
            bk = kc // 4
            mm(ps[:, bk, (kc % 4) * 128:(kc % 4 + 1) * 128], onesf[:], d[:], True, True, ["onesf", ("dg", kc % 2)], [("ps", bk)])
            if kc % 4 == 3:
                cp("dve", gate_b[:, bk * 512:(bk + 1) * 512], ps[:, bk, :], [], [("ps", bk), "gate_b"])
        for t in range(NT):
            P.op("act", lambda e, t=t: e.mul(out=X1[:, t, :], in_=X1[:, t, :], mul=ALPHA), [("X1", t)], [("X1", t)])
        wv = w_out.rearrange("(k p) n -> p k n", p=128)
        yv = ysc.rearrange("(k p) t -> p k t", p=128)
        tctr = 0
        for hh in range(2):
            for q in range(4):
                dma("sp", YTH[:, q * 4:(q + 1) * 4, :], yv[:, hh * 16 + q * 4: hh * 16 + (q + 1) * 4, :],
                    [("ysc", hh * 16 + q * 4 + i, n_) for i in range(4) for n_ in range(2)], [("YTH", q)])
            for j in range(4):
                slabs = [wload(wv[:, hh * 16 + s * 8: hh * 16 + (s + 1) * 8, j * 512:(j + 1) * 512], [128, 8, 512]) for s in range(2)]
                for t in [None]:
                    pass
                for tset in range(2):
                  for k in range(16):
                    slot, v = slabs[k // 8]
                    for t in range(tset * 4, tset * 4 + 4):
                        mm(ps[:, t, :], YTH[:, k, t * 128:(t + 1) * 128], v[:, k % 8, :], k == 0, k == 15,
                           [("YTH", k // 4), ("ring", slot)], [("ps", t)])
                  for t in range(tset * 4, tset * 4 + 4):
                    tb = tctr % 2
                    tctr += 1
                    sc = 1.0 if rstd_tile is None else rstd_tile[:, t:t + 1]
                    rk = [] if rstd_tile is None else ["rmsr"]
                    stt(tmp[tb][:], ps[:, t, :], sc, gate_b[:, j * 512:(j + 1) * 512], ALU.mult, ALU.mult,
                        ["gate_b"] + rk, [("ps", t), ("otmp", tb)])
                    tt("dve", X1[:, t, j * 512:(j + 1) * 512], X1[:, t, j * 512:(j + 1) * 512], tmp[tb][:], ALU.add,
                       [("X1", t), ("otmp", tb)], [("X1", t)])
        ydst = ya if unit == 0 else yb

        def O_stats(t):
            b = t % 2
            S = stat_tiles[b]
            kk = ("Ost", b)
            xk = ("X1", t)
            x_ap = X1[:, t, :]
            for c in range(4):
                P.op("dve", lambda e, c=c, S=S, x_ap=x_ap: e.bn_stats(out=S["st6"][:, c, :], in_=x_ap[:, c * 512:(c + 1) * 512]),
                     [xk], [(kk, "st6")])
            P.op("dve", lambda e, S=S: e.bn_aggr(out=S["mv"][:], in_=S["st6"][:].rearrange("p a b -> p (a b)")),
                 [(kk, "st6")], [(kk, "mv")])

        def O_norm(t):
            b = t % 2
            S = stat_tiles[b]
            kk = ("Ost", b)
            xk = ("X1", t)
            x_ap = X1[:, t, :]
            act(S["sd"][:], S["mv"][:, 1:2], AF.Sqrt, [(kk, "mv"), "epst"], [(kk, "sd")], bias=epst[:, 0:1], scale=1.0)
            P.op("dve", lambda e, S=S: e.reciprocal(out=S["rstd"][:], in_=S["sd"][:]), [(kk, "sd")], [(kk, "rstd")])
            stt(S["nmr"][:], S["mv"][:, 0:1], -1.0, S["rstd"][:], ALU.mult, ALU.mult, [(kk, "mv"), (kk, "rstd")], [(kk, "nmr")])
            act(x_ap, x_ap, AF.Identity, [xk, (kk, "rstd"), (kk, "nmr")], [xk], bias=S["nmr"][:, 0:1], scale=S["rstd"][:, 0:1])

        def O_affine(t):
            xk = ("X1", t)
            x_ap = X1[:, t, :]
            tt("dve", x_ap, x_ap, GB[:], ALU.mult, [xk, "GB"], [xk])
            tt("dve", x_ap, x_ap, BB[:], ALU.add, [xk, "BB"], [xk])
            if layer == DEPTH_RUN - 1:
                dma("sp", ydst[t * 128:(t + 1) * 128, :], x_ap, [xk], [("yout", t)])

        O_stats(0)
        for t in range(NT):
            O_norm(t)
            if t + 1 < NT:
                O_stats(t + 1)
            O_affine(t)
        P.barrier()


    RMSR = sb("RMSR", [128, NT], F32)
    SSQ = sb("SSQ", [128, NT, 8], F32)
    PH2 = off[0]

    def phase_S(unit):
        off[0] = PH2
        S_ = 4 if unit == 0 else 1
        L_ = 1024 // S_
        NCH = L_ // 128
        wv = ssd_w_in.rearrange("(k p) n -> p k n", p=128)
        LDH = sb("LDH", [128, NT, 128], BF16)
        LDL = sb("LDL", [128, NT, 128], BF16)
        DTt = sb("DTt", [128, 128], F32)
        DAb = sb("DAb", [128, NT, 128], BF16)
        NACS = sb("NACS", [128, 1, 128], F32)
        OFFD = sb("OFFD", [128, NT, 128], F32)
        CDEC = sb("CDEC", [128, NT, 128], F32)
        WST = sb("WST", [128, NT, 128], F32)
        dtb_b = sb("dtb_b", [128, 128], F32)
        A_b = sb("A_b", [128, 128], F32)
        dsk_b = sb("dsk_b", [128, 64], F32)
        DAf = [sb("DAf0", [128, 128], F32)] * 2
        TOTs = [sb("TOT0", [128, 128], F32)] * 2
        cvw = sb("cvw", [128, 48, 5], F32)
        cvb = sb("cvb", [128, 48], F32)
        nrw = sb("nrw", [128, 32], F32)
        PC = sb("PC", [128, S_ * (L_ + 4)], F32)
        ACC = sb("ACC", [128, 1024], F32)
        XCT1 = sb("XCT1", [128, 1024], BF16)
        BT = sb("BT", [128, 1024], BF16)
        CT = sb("CT", [128, 1024], BF16)
        XS = sb("XS", [128, NT, 512], BF16)
        Bt = sb("Bt", [128, NT, 128], BF16)
        ST = sb("ST", [128, 512], F32)
        Sbf = [sb("Sbf%d" % i, [128, 512], BF16) for i in range(2)]
        YFL1 = sb("YFL1", [128, 512], F32)
        alias_base = off[0]
        XDTD = [sb("XDTD%d" % i, [128, 512], BF16) for i in range(2)]
        Gd = [sb("Gd%d" % i, [128, 128], BF16) for i in range(2)]
        Lt = sb("Lt", [128, 1024], BF16)
        MT = [sb("MT%d" % i, [128, 1024], BF16) for i in range(2)]
        TMP = sb("TMPy", [128, 512], F32)
        YA = [sb("YA%d" % i, [128, 512], F32) for i in range(2)]
        YFL = sb("YFL", [128, 512], F32)
        STL = TMP[:, :].rearrange("p (q n) -> p q n", q=4)
        YFLs = [YFL, YFL1]
        YFLk = ["YFL", "YFL1"]
        SZ = [sb("SZ%d" % i, [128, 512], BF16) for i in range(2)]
        YG = [sb("YG%d" % i, [128, 512], BF16) for i in range(2)]
        YST = [sb("YST0", [128, 4, 128], BF16)] * 2
        print("phase_S sbuf end", off[0], "limit 229376")
        pcbytes = (S_ * (L_ + 4) * 4 + 31) // 32 * 32
        PC2 = sb("PC2", [128, S_ * (L_ + 4)], F32, at=alias_base)
        ACC2 = sb("ACC2", [128, 1024], F32, at=alias_base + pcbytes)
        assert alias_base + pcbytes + 4096 <= alias_base + 2 * 1024 + 2 * 256 + 2048 + 2 * 2048
        ALIAS = [("XDTD", 0), ("XDTD", 1), ("Gd", 0), ("Gd", 1), "Lt", ("MT", 0), ("MT", 1)]
        PCs = [PC, PC2]
        ACCs = [ACC, ACC2]
        pc3s = [x[:, :].rearrange("p (s l) -> p s l", s=S_) for x in PCs]
        acc3s = [x[:, :].rearrange("p (s l) -> p s l", s=S_) for x in ACCs]
        PCk = [["PC"], ["PC2"] + ALIAS]
        ACCk = [["ACC"], ["ACC2"] + ALIAS]
        dma("sp", dtb_b[:], dtb.to_broadcast([128, 128]), writes=["dtb_b"])
        dma("sp", A_b[:], alog.to_broadcast([128, 128]), writes=["A_b"])
        dma("sp", dsk_b[:], dsk.to_broadcast([128, 64]), writes=["dsk_b"])
        dma("sp", cvw[:], convw, writes=["cvw"])
        dma("sp", cvb[:], convb, writes=["cvb"])
        dma("sp", nrw[:], normw, writes=["nrw"])
        P.op("dve", lambda e: e.memset(PC[:], 0.0), writes=["PC"])
        P.op("dve", lambda e: e.memset(PC2[:], 0.0), writes=PCk[1])
        act(A_b[:], A_b[:], AF.Exp, ["A_b"], ["A_b"])
        ts("dve", A_b[:], A_b[:], -1.0, None, ALU.mult, None, ["A_b"], ["A_b"])
        sdt, vdt = wload(wv[:, :, 10240:10368], [128, 16, 128])
        for t in range(NT):
            bk = t % 2
            b2 = 0
            for k in range(16):
                mm(ps[:, bk, 0:128], HT[:, k, t * 128:(t + 1) * 128], vdt[:, k, :], k == 0, k == 15, [("ring", sdt)], [("ps", bk)])
            tt("dve", DTt[:], ps[:, bk, 0:128], dtb_b[:], ALU.add, ["dtb_b"], [("ps", bk), "DTt"])
            act(DTt[:], DTt[:], AF.Exp, ["DTt"], ["DTt"])
            act(DTt[:], DTt[:], AF.Ln, ["DTt", "onesf"], ["DTt"], bias=onesf[:, 0:1], scale=1.0)
            tt("dve", DAf[b2][:], DTt[:], A_b[:], ALU.mult, ["DTt", "A_b"], [("DAf", b2)])
            cp("dve", DAb[:, t, :], DAf[b2][:], [("DAf", b2)], [("DAb", t)])
            mm(ps[:, 2, 0:64], tff[:], DAf[b2][:, 0:64], True, True, ["tff", ("DAf", b2)], [("ps", 2)])
            mm(ps[:, 2, 64:128], tbf[:], DAf[b2][:, 64:128], True, True, ["tbf", ("DAf", b2)], [("ps", 2)])
            mm(ps[:, 2, 128:256], onesf[:], DAf[b2][:], True, True, ["onesf", ("DAf", b2)], [("ps", 2)])
            ts("dve", NACS[:, 0, :], ps[:, 2, 0:128], -1.0, None, ALU.mult, None, [], [("ps", 2), ("NACS", 0)])
            cp("dve", TOTs[b2][:], ps[:, 2, 128:256], [], [("ps", 2), ("TOT", b2)])
            act(OFFD[:, t, :], NACS[:, 0, :], AF.Exp, [("NACS", 0)], [("OFFD", t)], scale=-1.0)
            act(CDEC[:, t, :], TOTs[b2][:], AF.Exp, [("TOT", b2)], [("CDEC", t)])
            tt("dve", WST[:, t, :], TOTs[b2][:], NACS[:, 0, :], ALU.add, [("TOT", b2), ("NACS", 0)], [("WST", t)])
            act(WST[:, t, :], WST[:, t, :], AF.Exp, [("WST", t)], [("WST", t)])
            tt("dve", WST[:, t, :], WST[:, t, :], DTt[:], ALU.mult, [("WST", t), "DTt"], [("WST", t)])
            act(DTt[:], DTt[:], AF.Ln, ["DTt", ("WST", t)], ["DTt"])
            cp("dve", LDH[:, t, :], DTt[:], ["DTt"], [("LDH", t)])
            tt("dve", DTt[:], DTt[:], LDH[:, t, :], ALU.subtract, ["DTt", ("LDH", t)], ["DTt"])
            cp("dve", LDL[:, t, :], DTt[:], ["DTt"], [("LDL", t)])
        pv = ps[:, 2, :].bitcast(BF16)
        pre_xs = [None]
        for g in range(8):
            cols = [(4096 + g * 512, 256), (4096 + g * 512 + 256, 256), (8192 + g * 128, 128), (9216 + g * 128, 128)]
            if pre_xs[0] is not None:
                wsl = pre_xs[0] + [wload(wv[:, :, c0:c0 + w], [128, 16, w]) for (c0, w) in cols[2:]]
                pre_xs[0] = None
            else:
                wsl = [wload(wv[:, :, c0:c0 + w], [128, 16, w]) for (c0, w) in cols]

            def inproj_mm(c):
                slot, v = wsl[c // 2] if c < 4 else wsl[c - 2]
                mo = (c % 2) * 128 if c < 4 else 0
                for n in range(2):
                    bk = n if c % 2 == 0 else 3 + n
                    for k in range(16):
                        mm(ps[:, bk, :], v[:, k, mo:mo + 128], HT[:, k, n * 512:(n + 1) * 512], k == 0, k == 15,
                           [("ring", slot)], [("ps", bk)])

            def inproj_evac(c):
                pc3 = pc3s[c % 2]
                if c % 2 == 1:
                    P.op("dve", lambda e: e.memset(pc3s[1][:, :, 0:2], 0.0), writes=PCk[1])
                    P.op("dve", lambda e: e.memset(pc3s[1][:, :, L_ + 2:L_ + 4], 0.0), writes=PCk[1])
                for n in range(2):
                    bk = n if c % 2 == 0 else 3 + n
                    if unit == 0:
                        cp("act", pc3[:, 2 * n:2 * n + 2, 2:L_ + 2], ps[:, bk, :].rearrange("p (s l) -> p s l", s=2),
                           [], [("ps", bk)] + PCk[c % 2])
                    else:
                        cp("act", pc3[:, 0, 2 + n * 512:2 + (n + 1) * 512], ps[:, bk, :], [], [("ps", bk)] + PCk[c % 2])

            inproj_mm(0)
            inproj_evac(0)
            for c in range(6):
                if c + 1 < 6:
                    inproj_mm(c + 1)
                ctile = (g * 4 + c) if c < 4 else (32 + g if c == 4 else 40 + g)
                pc3 = pc3s[c % 2]
                acc3 = acc3s[c % 2]
                pk = PCk[c % 2]
                ak = ACCk[c % 2]
                ts("dve", acc3, pc3[:, :, 0:L_], cvw[:, ctile, 0:1], cvb[:, ctile:ctile + 1], ALU.mult, ALU.add,
                   ["cvw", "cvb"] + (pk if c % 2 == 0 else []), ak + (pk if c % 2 == 1 else []))
                for kk in range(1, 5):
                    stt(acc3, pc3[:, :, kk:kk + L_], cvw[:, ctile, kk:kk + 1], acc3, ALU.mult, ALU.add,
                        ["cvw"] + (pk if c % 2 == 0 else []), ak + (pk if c % 2 == 1 else []))
                dstT = XCT1 if c < 4 else (BT if c == 4 else CT)
                dkey = "XCT1" if c < 4 else ("BT" if c == 4 else "CT")
                act(dstT[:], ACCs[c % 2][:], AF.Silu, ak if c % 2 == 0 else [], [dkey] + (ak if c % 2 == 1 else []))
                if c < 5:
                    for t in range(NT):
                        tr(pv[:, t * 128:(t + 1) * 128], dstT[:, t * 128:(t + 1) * 128], identb[:], [dkey, "identb"], [("ps", 2)])
                    src8 = pv[:, :].rearrange("p (t q) -> p t q", t=NT)
                    if c < 4:
                        cp("act", XS[:, :, c * 128:(c + 1) * 128], src8, [], [("ps", 2), "XS"])
                    else:
                        cp("act", Bt[:, :, :], src8, [], [("ps", 2), "Bt"])
                if c + 1 < 6:
                    inproj_evac(c + 1)
            wz = [wload(wv[:, :, g * 512 + hz * 256: g * 512 + (hz + 1) * 256], [128, 16, 256]) for hz in range(2)]
            if g + 1 < 8:
                pre_xs[0] = [wload(wv[:, :, c0:c0 + 256], [128, 16, 256])
                             for c0 in (4096 + (g + 1) * 512, 4096 + (g + 1) * 512 + 256)]

            iters = []
            for d in range(2):
                for sq in range(S_):
                    order = list(range(NCH)) if d == 0 else list(range(NCH - 1, -1, -1))
                    for ii, ch in enumerate(order):
                        iters.append((d, sq, ch, ii == 0, ii == NCH - 1))

            def front_a(i):
                d, sq, ch, first, last = iters[i]
                p = i % 2
                t = sq * NCH + ch
                T_d, neg_d, mask_d = (tfb, negf, maskf) if d == 0 else (tbb, negb, maskb)
                Tk, nk, mk = ("tfb", "negf", "maskf") if d == 0 else ("tbb", "negb", "maskb")
                nT_d, nTk = (ntfb, "ntfb") if d == 0 else (ntbb, "ntbb")
                c0 = d * 64 + g * 8
                wsv = WST[:, t, c0:c0 + 8].unsqueeze(2).to_broadcast([128, 8, 64])
                xs3 = XS[:, t, :].rearrange("p (h q) -> p h q", h=8)
                tt("pool", XDTD[p][:].rearrange("p (h q) -> p h q", h=8), xs3, wsv, ALU.mult, ["XS", ("WST", t)], [("XDTD", p)])
                mm(ps[:, 3, 0:128], BT[:, t * 128:(t + 1) * 128], CT[:, t * 128:(t + 1) * 128], True, True, ["BT", "CT"], [("ps", 3)])
                tt("dve", Gd[p][:], ps[:, 3, 0:128], mask_d[:], ALU.mult, [mk], [("ps", 3), ("Gd", p)])
                for hb in range(2):
                    bk = 4 + hb
                    for hh in range(4):
                        col = c0 + hb * 4 + hh
                        dab = DAb[:, t, col:col + 1].to_broadcast([128, 128])
                        mm(ps[:, bk, hh * 128:(hh + 1) * 128], identb[:], neg_d[:], True, False, ["identb", nk], [("ps", bk)])
                        mm(ps[:, bk, hh * 128:(hh + 1) * 128], dab, T_d[:], False, False, [("DAb", t), Tk], [("ps", bk)])
                        mm(ps[:, bk, hh * 128:(hh + 1) * 128], nT_d[:], dab, False, False, [("DAb", t), nTk], [("ps", bk)])
                        mm(ps[:, bk, hh * 128:(hh + 1) * 128], identb[:], LDH[:, t, col:col + 1].to_broadcast([128, 128]), False, False,
                           ["identb", ("LDH", t)], [("ps", bk)])
                        mm(ps[:, bk, hh * 128:(hh + 1) * 128], identb[:], LDL[:, t, col:col + 1].to_broadcast([128, 128]), False, True,
                           ["identb", ("LDL", t)], [("ps", bk)])
                for hb in range(2):
                    bk = 4 + hb
                    act(Lt[:, hb * 512:(hb + 1) * 512], ps[:, bk, :], AF.Exp, [], [("ps", bk), "Lt"])

            def front_b(i):
                p = i % 2
                tt("dve", MT[p][:].rearrange("p (h l) -> p h l", h=8), Lt[:].rearrange("p (h l) -> p h l", h=8),
                   Gd[p][:].unsqueeze(1).to_broadcast([128, 8, 128]), ALU.mult, ["Lt", ("Gd", p)], [("MT", p)])

            def back(i):
                d, sq, ch, first, last = iters[i]
                p = i % 2
                t = sq * NCH + ch
                c0 = d * 64 + g * 8
                xs3 = XS[:, t, :].rearrange("p (h q) -> p h q", h=8)
                zfirst = first and unit == 0
                if first:
                    if unit == 0:
                        pass
                    else:
                        r0 = d * 4096 + g * 512
                        dma("sp", STL, st_in[r0:r0 + 512, :].rearrange("(q p) n -> p q n", p=128), writes=["TMP"])
                        for q in range(4):
                            tr(ps[:, 2, q * 128:(q + 1) * 128], STL[:, q, :], identf[:], ["TMP", "identf"], [("ps", 2)])
                        cp("dve", ST[:], ps[:, 2, :], [], [("ps", 2), "ST"])
                        cp("dve", Sbf[i % 2][:], ST[:], ["ST"], [("Sbf", i % 2)])
                for hl in range(8):
                    mm(ps[:, 6, hl * 64:(hl + 1) * 64], MT[p][:, hl * 128:(hl + 1) * 128], XS[:, t, hl * 64:(hl + 1) * 64], True, True,
                       [("MT", p), "XS"], [("ps", 6)])
                if not zfirst:
                    mm(ps[:, 7, :], CT[:, t * 128:(t + 1) * 128], Sbf[i % 2][:], True, True, ["CT", ("Sbf", i % 2)], [("ps", 7)])
                mm(ps[:, 3, :], Bt[:, t, :], XDTD[p][:], True, True, ["Bt", ("XDTD", p)], [("ps", 3)])
                if zfirst:
                    cp("dve", ST[:], ps[:, 3, :], [], [("ps", 3), "ST"])
                else:
                    tt("dve", ST[:].rearrange("p (h q) -> p h q", h=8), ST[:].rearrange("p (h q) -> p h q", h=8),
                       CDEC[:, t, c0:c0 + 8].unsqueeze(2).to_broadcast([128, 8, 64]), ALU.mult, ["ST", ("CDEC", t)], ["ST"])
                    tt("dve", ST[:], ST[:], ps[:, 3, :], ALU.add, ["ST"], [("ps", 3), "ST"])
                if not last:
                    cp("act", Sbf[(i + 1) % 2][:], ST[:], ["ST"], [("Sbf", (i + 1) % 2)])
                if not zfirst:
                    tt("dve", TMP[:].rearrange("p (h q) -> p h q", h=8), ps[:, 7, :].rearrange("p (h q) -> p h q", h=8),
                       OFFD[:, t, c0:c0 + 8].unsqueeze(2).to_broadcast([128, 8, 64]), ALU.mult, [("OFFD", t)], [("ps", 7), "TMP"])
                ya = YA[p]
                yk = ("YA", p)
                if zfirst:
                    cp("dve", ya[:], ps[:, 6, :], [], [("ps", 6), yk])
                else:
                    tt("dve", ya[:], ps[:, 6, :], TMP[:], ALU.add, ["TMP"], [("ps", 6), yk])
                ydst_ = yfs if d == 0 else yfs2
                dma("sp", ydst_[t * 128:(t + 1) * 128, :], ya[:], [yk], [("yfs", d, t)])
                if last and unit == 0:
                    for q in range(4):
                        tr(ps[:, 2, q * 128:(q + 1) * 128], ST[:, q * 128:(q + 1) * 128], identf[:], ["ST", "identf"], [("ps", 2)])
                    cp("act", TMP[:], ps[:, 2, :], [], [("ps", 2), "TMP"])
                    r0 = sq * 8192 + d * 4096 + g * 512
                    dma("sp", so[r0:r0 + 512, :].rearrange("(q p) n -> p q n", p=128), STL, ["TMP"], [("so", sq, d, g)])

            for i in range(len(iters)):
                front_a(i)
                if i > 0:
                    back(i - 1)
                front_b(i)
            back(len(iters) - 1)
            def zproj(t):
                bkz = t % 2
                for hz in range(2):
                    slot, v = wz[hz]
                    for k in range(16):
                        mm(ps[:, bkz, hz * 256:(hz + 1) * 256], HT[:, k, t * 128:(t + 1) * 128], v[:, k, :], k == 0, k == 15,
                           [("ring", slot)], [("ps", bkz)])

            zproj(0)
            for t0 in range(2):
                dma("sp", YA[t0][:], yfs[t0 * 128:(t0 + 1) * 128, :], [("yfs", 0, t0)], [("YA", t0)])
                dma("sp", YFLs[t0][:], yfs2[t0 * 128:(t0 + 1) * 128, :], [("yfs", 1, t0)], [YFLk[t0]])
            for t in range(NT):
                p = t % 2
                ya = YA[p]
                yk = ("YA", p)
                bkz = t % 2
                act(SZ[p][:], ps[:, bkz, :], AF.Silu, [], [("ps", bkz), ("SZ", p)])
                if t + 1 < NT:
                    zproj(t + 1)
                tt("pool", TMP[:].rearrange("p (h q) -> p h q", h=8), XS[:, t, :].rearrange("p (h q) -> p h q", h=8),
                   dsk_b[:, g * 8:(g + 1) * 8].unsqueeze(2).to_broadcast([128, 8, 64]), ALU.mult, ["XS", "dsk_b"], ["TMP"])
                tt("dve", ya[:], ya[:], YFLs[p][:], ALU.add, [yk, YFLk[p]], [yk])
                tt("dve", ya[:], ya[:], TMP[:], ALU.add, [yk, "TMP"], [yk])
                tt("dve", YG[p][:], ya[:], SZ[p][:], ALU.mult, [yk, ("SZ", p)], [("YG", p)])
                if t + 2 < NT:
                    dma("sp", ya[:], yfs[(t + 2) * 128:(t + 3) * 128, :], [("yfs", 0, t + 2)], [yk])
                    dma("sp", YFLs[p][:], yfs2[(t + 2) * 128:(t + 3) * 128, :], [("yfs", 1, t + 2)], [YFLk[p]])
                P.op("dve", lambda e, t=t, g=g, p=p: e.scalar_tensor_tensor(out=SZ[p][:], in0=YG[p][:], scalar=1.0, in1=YG[p][:],
                                                                          op0=ALU.mult, op1=ALU.mult, accum_out=SSQ[:, t, g:g + 1]),
                     [("YG", p)], [("SZ", p), ("SSQ", t, g)])
                bkt = 6 + p
                pvt = ps[:, bkt, :].bitcast(BF16)
                for c in range(4):
                    tr(pvt[:, c * 128:(c + 1) * 128], YG[p][:, c * 128:(c + 1) * 128], identb[:], [("YG", p), "identb"], [("ps", bkt)])
                for c in range(4):
                    act(YST[p][:, c, :], pvt[:, c * 128:(c + 1) * 128], AF.Copy, ["nrw"], [("ps", bkt), ("YST", 0)],
                        scale=nrw[:, g * 4 + c:g * 4 + c + 1])
                dma("sp", ysc.rearrange("(c p) t -> p c t", p=128)[:, g * 4:(g + 1) * 4, t * 128:(t + 1) * 128], YST[p][:],
                    [("YST", 0)], [("ysc", g, t)])
        P.op("dve", lambda e: e.reduce_sum(out=RMSR[:], in_=SSQ[:], axis=mybir.AxisListType.X),
             [("SSQ", t, g) for t in range(NT) for g in range(8)], ["rmsr"])
        act(RMSR[:], RMSR[:], AF.Sqrt, ["rmsr", "epst"], ["rmsr"], bias=epst[:, 0:1], scale=1.0 / 4096.0)
        P.op("dve", lambda e: e.reciprocal(out=RMSR[:], in_=RMSR[:]), ["rmsr"], ["rmsr"])
        P.barrier()

    STAGE = int(os.environ.get("MK_STAGE", "99"))
    if MKM >= 4:
        P.barrier()
    for unit in range(2):
        phase_L(0, unit)
        phase_F(unit)
        phase_O(0, unit, fno_w_out, None)
        if DEPTH_RUN >= 2:
            phase_L(1, unit)
            phase_S(unit)
            phase_O(1, unit, ssd_w_out, RMSR)

    build_and_emit(nc, P)
    return nc


def _pos_table():
    def sincos(p, dim):
        omega = 1.0 / (10000.0 ** (np.arange(dim // 2, dtype=np.float32) / np.float32(dim / 2)))
        ang = p.astype(np.float32)[:, None] * omega[None, :].astype(np.float32)
        return np.concatenate([np.sin(ang), np.cos(ang)], axis=-1).astype(np.float32)
    t = np.arange(1024)
    row, col = t // 64, t % 64
    return np.concatenate([sincos(row, 1024), sincos(col, 1024)], axis=-1).astype(np.float32)


def _consts():
    bf = ml_dtypes.bfloat16
    i = np.arange(128)
    c = {}
    c["c_identb"] = np.eye(128, dtype=np.float32).astype(bf)
    c["c_identf"] = np.eye(128, dtype=np.float32)
    tf = (i[:, None] <= i[None, :]).astype(np.float32)
    tb = (i[:, None] >= i[None, :]).astype(np.float32)
    c["c_tfb"] = tf.astype(bf)
    c["c_tbb"] = tb.astype(bf)
    c["c_tff"] = tf
    c["c_tbf"] = tb
    NEG = -30000.0
    c["c_negf"] = (NEG * (i[None, :] < i[:, None])).astype(np.float32).astype(bf)
    c["c_negb"] = (NEG * (i[None, :] > i[:, None])).astype(np.float32).astype(bf)
    c["c_maskf"] = (i[None, :] >= i[:, None]).astype(np.float32).astype(bf)
    c["c_maskb"] = (i[None, :] <= i[:, None]).astype(np.float32).astype(bf)
    c["c_onesf"] = np.ones((128, 128), np.float32)
    c["c_ntfb"] = (-tf).astype(bf)
    c["c_ntbb"] = (-tb).astype(bf)
    j = np.arange(256)
    ang = 2.0 * np.pi * ((j[:, None] * j[None, :]) % 256) / 256.0
    cs = np.concatenate([np.cos(ang), np.sin(ang)], axis=1) / 16.0
    c["c_csc"] = cs.reshape(2, 128, 512).transpose(1, 0, 2).astype(np.float32).astype(bf)
    cl = np.cos(ang) / 16.0
    sl = -np.sin(ang) / 16.0
    d256 = np.stack([cl, sl], axis=0)
    c["c_dft256"] = d256.reshape(2, 2, 128, 256).transpose(2, 1, 0, 3).astype(np.float32).astype(bf)
    j = np.arange(1024)
    ang = 2.0 * np.pi * ((j[:, None] * j[None, :]) % 1024) / 1024.0
    c["c_dft1024"] = np.stack([np.cos(ang) / 32.0, -np.sin(ang) / 32.0], axis=0).astype(np.float32).astype(bf)
    return c


_NC_CACHE = {}


def kernel(x_prompt, x_sample, state_ssd_ctx, c, c_ctx, w_ada, b_ada, ln_g, ln_b, fno_w_in, fno_w_out,
           ssd_w_in, ssd_conv_w, ssd_conv_b, ssd_dt_bias, ssd_a_log, ssd_d, ssd_norm_w, ssd_w_out):
    f = lambda a: np.ascontiguousarray(np.asarray(a, dtype=np.float32))
    x_prompt, x_sample, state_ssd_ctx, c, c_ctx = map(f, (x_prompt, x_sample, state_ssd_ctx, c, c_ctx))
    if "nc" not in _NC_CACHE:
        _NC_CACHE["nc"] = build_nc()
        _NC_CACHE["consts"] = _consts()
        _NC_CACHE["pos"] = _pos_table()
    nc = _NC_CACHE["nc"]
    consts = _NC_CACHE["consts"]
    shared = dict(consts)
    shared["pos"] = _NC_CACHE["pos"]
    shared["w_ada"] = f(w_ada)
    shared["bT"] = f(np.asarray(b_ada).reshape(2, 48, 128).transpose(2, 0, 1))
    shared["ln_g"] = f(ln_g)
    shared["ln_b"] = f(ln_b)
    shared["fno_w_in"] = f(fno_w_in[0])
    shared["fno_w_out"] = f(fno_w_out[0])
    shared["ssd_w_in"] = f(ssd_w_in[0])
    shared["ssd_w_out"] = f(ssd_w_out[0])
    shared["convw"] = f(np.asarray(ssd_conv_w[0]).reshape(5, 48, 128).transpose(2, 1, 0))
    shared["convb"] = f(np.asarray(ssd_conv_b[0]).reshape(48, 128).T)
    shared["dtb"] = f(np.asarray(ssd_dt_bias[0]).reshape(1, 128))
    shared["alog"] = f(np.asarray(ssd_a_log[0]).reshape(1, 128))
    shared["dsk"] = f(np.asarray(ssd_d[0]).reshape(1, 64))
    shared["normw"] = f(np.asarray(ssd_norm_w[0]).reshape(32, 128).T)
    in_maps = []
    import os
    NCORE = int(os.environ.get("MK_CORES", "8"))
    for core in range(NCORE):
        m = dict(shared)
        m["xa"] = x_prompt[4 * core:4 * core + 4].reshape(1024, D)
        m["xb"] = x_sample[core]
        m["st_in"] = state_ssd_ctx[core, 0].reshape(2 * 64 * 64, 128)
        cond = np.stack([c_ctx, c[core]], axis=0)
        m["cT"] = f(cond.reshape(2, 16, 128).transpose(2, 1, 0))
        in_maps.append(m)
    res = run_bass_kernel_spmd(nc, in_maps, core_ids=list(range(NCORE)))
    r = res.results
    y_prompt = np.concatenate([r[i]["ya"].reshape(4, 256, D) for i in range(NCORE)], axis=0)
    y_sample = np.stack([r[i]["yb"] for i in range(NCORE)], axis=0)
    state = np.concatenate([r[i]["so"].reshape(4, 1, 2, 64, 64, 128) for i in range(NCORE)], axis=0)
    return (y_prompt.astype(np.float32), y_sample.astype(np.float32), state.astype(np.float32))
```
